# Optimizing a Trainium2 kernel written in Bass

```python
import jax, jax.numpy as jnp
from jax import lax
import numpy as np

D_MODEL = 1024
BATCH = 8
SEQ = 2048
DEPTH = 1
DEC_BATCH = 128
DEC_SEQ = 1
PAST_LEN = 16384
PAGE_SIZE = 128

D_A = D_MODEL
H_A = 4
DH_A = D_A // H_A
CONV_W = 4
D_B = D_MODEL
G_B = 4
DG_B = D_B // G_B
CHUNK = 128
D_PLE = 256
EPS = 1e-6

IN_SPLITS = (2 * D_A, D_A, D_A, D_A, H_A, H_A, D_B, D_B, D_B, D_MODEL, D_MODEL)
N_IN = 5 * D_A + 2 * H_A + 3 * D_B + 2 * D_MODEL

kernel_name = 'hybrid_mlstm_chunkmlp_decode_step'


def rmsnorm(x, g):
    xf = x.astype(jnp.float32)
    y = xf * lax.rsqrt(jnp.mean(xf * xf, axis=-1, keepdims=True) + EPS)
    return (y * g.astype(jnp.float32)).astype(x.dtype)


def layernorm(x, g, b):
    xf = x.astype(jnp.float32)
    mu = jnp.mean(xf, axis=-1, keepdims=True)
    xc = xf - mu
    y = xc * lax.rsqrt(jnp.mean(xc * xc, axis=-1, keepdims=True) + EPS)
    return (y * g.astype(jnp.float32) + b.astype(jnp.float32)).astype(x.dtype)


def split_in(proj):
    out, start = [], 0
    for size in IN_SPLITS:
        out.append(proj[..., start:start + size])
        start += size
    return out


def causal_conv(prev, x, w, b):
    T = x.shape[1]
    xp = jnp.concatenate([prev.astype(x.dtype), x], axis=1)
    y = b
    for j in range(CONV_W):
        y = y + xp[:, j:j + T] * w[j]
    return y, xp[:, T:]


def mlstm_chunk(carry, inp):
    C, n, m = carry
    q, k, v, ig, lf = inp
    L = q.shape[1]
    F = jnp.cumsum(lf, axis=1)
    a = ig - F
    m_t = F + jnp.maximum(m[:, None, :], lax.cummax(a, axis=1))
    causal = jnp.tril(jnp.ones((L, L), dtype=bool))
    logd = jnp.transpose(F - m_t, (0, 2, 1))[:, :, :, None] + jnp.transpose(a, (0, 2, 1))[:, :, None, :]
    d = jnp.exp(jnp.where(causal, logd, -jnp.inf))
    s = jnp.einsum('bthd,bshd->bhts', q, k) * d
    w_prev = jnp.exp(F + m[:, None, :] - m_t)
    num = jnp.einsum('bhts,bshd->bthd', s, v) + w_prev[..., None] * jnp.einsum('bthk,bhkv->bthv', q, C)
    den = jnp.transpose(jnp.sum(s, axis=-1), (0, 2, 1)) + w_prev * jnp.einsum('bthk,bhk->bth', q, n)
    h = num / jnp.maximum(jnp.abs(den), jnp.exp(-m_t))[..., None]
    m_new = m_t[:, -1]
    f_state = jnp.exp(F[:, -1] + m - m_new)
    w_src = jnp.exp(F[:, -1:, :] + a - m_new[:, None, :])
    C_new = f_state[..., None, None] * C + jnp.einsum('bsh,bshk,bshv->bhkv', w_src, k, v)
    n_new = f_state[..., None] * n + jnp.einsum('bsh,bshk->bhk', w_src, k)
    return (C_new, n_new, m_new), h


def mlstm(q, k, v, ig, lf, C0, n0, m0, n_chunks):
    B, T, H, D = q.shape
    L = T // n_chunks

    def to_chunks(t):
        return jnp.moveaxis(t.reshape((B, n_chunks, L) + t.shape[2:]), 1, 0)

    (C, n, m), h = lax.scan(mlstm_chunk, (C0, n0, m0),
                            (to_chunks(q), to_chunks(k), to_chunks(v), to_chunks(ig), to_chunks(lf)))
    return jnp.moveaxis(h, 0, 1).reshape(B, T, H, D), C, n, m


def chunk_mlp(u, vn, w_s, b_s, n_chunks):
    B, T, _ = u.shape
    L = T // n_chunks
    wm = jnp.tril(w_s[:, :L, :L])
    vc = vn.reshape(B, n_chunks, L, G_B, DG_B)
    mixed = jnp.einsum('gts,bcsgd->bctgd', wm, vc) + jnp.transpose(b_s[:, :L])[None, None, :, :, None]
    return u * mixed.reshape(B, T, D_B)


def mixer_layer(x, p, conv_prev, C0, n0, m0, n_chunks,
                g_pre, w_in, b_gates, conv_w, conv_b, g_head, g_lnv, b_lnv, w_s, b_s,
                w_pa, w_pb, w_out, g_post, w_ple, w_ple_gate, g_ple):
    B, T, _ = x.shape
    f32 = jnp.float32
    h = rmsnorm(x, g_pre)
    proj = h @ w_in
    qk_pre, v, o, z_a, ig, fg, u, vb, z_b, gate_a, gate_b = split_in(proj)
    qk, conv_new = causal_conv(conv_prev, qk_pre, conv_w, conv_b)
    qk = jax.nn.silu(qk)
    q = qk[..., :D_A].reshape(B, T, H_A, DH_A).astype(f32)
    k = (qk[..., D_A:] * DH_A ** -0.5).reshape(B, T, H_A, DH_A).astype(f32)
    vh = v.reshape(B, T, H_A, DH_A).astype(f32)
    ig_t = (ig + b_gates[:H_A]).astype(f32)
    lf = jax.nn.log_sigmoid((fg + b_gates[H_A:]).astype(f32))
    hm, C, n, m = mlstm(q, k, vh, ig_t, lf, C0.astype(f32), n0.astype(f32), m0.astype(f32), n_chunks)
    hm = rmsnorm(hm, g_head.reshape(H_A, DH_A)).reshape(B, T, D_A).astype(x.dtype)
    y_a = jax.nn.sigmoid(o) * hm * jax.nn.silu(z_a)
    vn = layernorm(vb, g_lnv, b_lnv)
    y_b = chunk_mlp(u, vn, w_s, b_s, n_chunks) * jax.nn.silu(z_b)
    merged = jax.nn.sigmoid(gate_a) * (y_a @ w_pa) + jax.nn.sigmoid(gate_b) * (y_b @ w_pb)
    x = x + rmsnorm(merged @ w_out, g_post)
    x = x + jax.nn.sigmoid(x @ w_ple_gate) * rmsnorm(p @ w_ple, g_ple)
    return x, C, n, m, conv_new, vn


def setup_inputs(seed: int = 0) -> dict:
    key = jax.random.key(seed)
    ks = jax.random.split(key, 32)

    def nrm(k, shape, s):
        return jax.random.normal(k, shape, jnp.float32) * s

    b_gates = jnp.concatenate([
        nrm(ks[10], (DEPTH, H_A), 0.1),
        jnp.linspace(3.0, 6.0, H_A, dtype=jnp.float32)[None, :] + nrm(ks[11], (DEPTH, H_A), 0.1)], axis=-1)
    return {
        'x_prompt': nrm(ks[0], (BATCH, SEQ, D_MODEL), 1.0),
        'x_sample': nrm(ks[1], (DEC_BATCH, DEC_SEQ, D_MODEL), 1.0),
        'state_C': nrm(ks[2], (DEPTH, DEC_BATCH, H_A, DH_A, DH_A), DH_A ** -0.5),
        'state_n': nrm(ks[3], (DEPTH, DEC_BATCH, H_A, DH_A), DH_A ** -0.5),
        'state_m': nrm(ks[4], (DEPTH, DEC_BATCH, H_A), 1.0),
        'state_conv': nrm(ks[5], (DEPTH, DEC_BATCH, CONV_W - 1, 2 * D_A), 1.0),
        'p_prompt': nrm(ks[6], (DEPTH, BATCH, SEQ, D_PLE), 1.0),
        'p_sample': nrm(ks[7], (DEPTH, DEC_BATCH, DEC_SEQ, D_PLE), 1.0),
        'g_pre': 1.0 + nrm(ks[8], (DEPTH, D_MODEL), 0.01),
        'w_in': nrm(ks[9], (DEPTH, D_MODEL, N_IN), D_MODEL ** -0.5),
        'b_gates': b_gates,
        'conv_w': nrm(ks[12], (DEPTH, CONV_W, 2 * D_A), CONV_W ** -0.5),
        'conv_b': nrm(ks[13], (DEPTH, 2 * D_A), 0.01),
        'g_head': 1.0 + nrm(ks[14], (DEPTH, D_A), 0.01),
        'g_lnv': 1.0 + nrm(ks[15], (DEPTH, D_B), 0.01),
        'b_lnv': nrm(ks[16], (DEPTH, D_B), 0.01),
        'w_s': nrm(ks[17], (DEPTH, G_B, CHUNK, CHUNK), CHUNK ** -0.5),
        'b_s': 1.0 + nrm(ks[18], (DEPTH, G_B, CHUNK), 0.1),
        'w_pa': nrm(ks[19], (DEPTH, D_A, D_MODEL), D_A ** -0.5),
        'w_pb': nrm(ks[20], (DEPTH, D_B, D_MODEL), D_B ** -0.5),
        'w_out': nrm(ks[21], (DEPTH, D_MODEL, D_MODEL), D_MODEL ** -0.5),
        'g_post': 1.0 + nrm(ks[22], (DEPTH, D_MODEL), 0.01),
        'w_ple': nrm(ks[23], (DEPTH, D_PLE, D_MODEL), D_PLE ** -0.5),
        'w_ple_gate': nrm(ks[24], (DEPTH, D_MODEL, D_MODEL), D_MODEL ** -0.5),
        'g_ple': 1.0 + nrm(ks[25], (DEPTH, D_MODEL), 0.01),
    }


def reference(x_prompt, x_sample, state_C, state_n, state_m, state_conv, p_prompt, p_sample,
              g_pre, w_in, b_gates, conv_w, conv_b, g_head, g_lnv, b_lnv, w_s, b_s,
              w_pa, w_pb, w_out, g_post, w_ple, w_ple_gate, g_ple):
    B = x_prompt.shape[0]
    f32 = jnp.float32
    xp, xs = x_prompt, x_sample
    Cp_l, np_l, mp_l, cvp_l = [], [], [], []
    Cs_l, ns_l, ms_l, cvs_l, vs_l = [], [], [], [], []
    for l in range(DEPTH):
        wl = (g_pre[l], w_in[l], b_gates[l], conv_w[l], conv_b[l], g_head[l], g_lnv[l], b_lnv[l],
              w_s[l], b_s[l], w_pa[l], w_pb[l], w_out[l], g_post[l], w_ple[l], w_ple_gate[l], g_ple[l])
        conv0 = jnp.zeros((B, CONV_W - 1, 2 * D_A), xp.dtype)
        C0 = jnp.zeros((B, H_A, DH_A, DH_A), f32)
        n0 = jnp.zeros((B, H_A, DH_A), f32)
        m0 = jnp.zeros((B, H_A), f32)
        xp, Cp, n_p, mp, cvp, _ = mixer_layer(xp, p_prompt[l], conv0, C0, n0, m0, SEQ // CHUNK, *wl)
        xs, Cs, n_s, ms, cvs, vs = mixer_layer(xs, p_sample[l], state_conv[l], state_C[l], state_n[l],
                                               state_m[l], 1, *wl)
        Cp_l.append(Cp); np_l.append(n_p); mp_l.append(mp); cvp_l.append(cvp)
        Cs_l.append(Cs); ns_l.append(n_s); ms_l.append(ms); cvs_l.append(cvs); vs_l.append(vs)
    return (xp, xs,
            jnp.stack(Cp_l), jnp.stack(np_l), jnp.stack(mp_l), jnp.stack(cvp_l),
            jnp.stack(Cs_l), jnp.stack(ns_l), jnp.stack(ms_l), jnp.stack(cvs_l), jnp.stack(vs_l))
```

```python
from contextlib import ExitStack
import numpy as np
import ml_dtypes
import concourse.bass as bass
import concourse.mybir as mybir
from concourse.bass_utils import run_bass_kernel_spmd

F32 = mybir.dt.float32
BF16 = mybir.dt.bfloat16
AF = mybir.ActivationFunctionType
ALU = mybir.AluOpType
AX = mybir.AxisListType

NT = 2064
EPS = 1e-6
C_QK, C_V, C_O, C_ZA, C_G, C_U, C_VB, C_ZB, C_GA, C_GB = 0, 2048, 3072, 4096, 5120, 5128, 6152, 7176, 8200, 9224


class Prog:
    def __init__(self, nc, n_dma_sems=40):
        self.nc = nc
        self.engs = {"pe": nc.tensor, "act": nc.scalar, "dve": nc.vector,
                     "pool": nc.gpsimd, "sp": nc.sync}
        self.psem = {e: nc.alloc_semaphore("prog_" + e) for e in self.engs}
        self.cnt = {e: 0 for e in self.engs}
        self.dsem = [nc.alloc_semaphore("dma_%d" % i) for i in range(n_dma_sems)]
        self.dcnt = [0] * n_dma_sems
        self.dnext = 0
        self.seen = {e: {} for e in self.engs}
        self.last_w = {}
        self.readers = {}
        self.semobj = {}
        for e in self.engs:
            self.semobj[("p", e)] = self.psem[e]
        for i, s in enumerate(self.dsem):
            self.semobj[("d", i)] = s

    def _deps(self, reads, writes):
        deps = []
        for r in reads:
            if r in self.last_w:
                deps.append(self.last_w[r])
        for w in writes:
            if w in self.last_w:
                deps.append(self.last_w[w])
            deps.extend(self.readers.get(w, []))
        return deps

    def _wait(self, eng, deps, same_engine=True):
        best = {}
        for (sid, val) in deps:
            if sid == ("p", eng) and not same_engine:
                continue
            if best.get(sid, 0) < val:
                best[sid] = val
        for sid, val in best.items():
            if self.seen[eng].get(sid, 0) >= val:
                continue
            self.engs[eng].wait_ge(self.semobj[sid], val)
            self.seen[eng][sid] = val

    def _record(self, tok, reads, writes):
        for w in writes:
            self.last_w[w] = tok
            self.readers[w] = []
        for r in reads:
            self.readers.setdefault(r, []).append(tok)

    @staticmethod
    def _psum_excl(reads, writes):
        extra = [k for k in reads if k.startswith("ps") and k[2:3].isdigit()]
        return list(reads), list(writes) + extra

    def op(self, eng, fn, reads=(), writes=(), same_engine=True):
        reads, writes = self._psum_excl(reads, writes)
        deps = self._deps(reads, writes)
        self._wait(eng, deps, same_engine)
        ins = fn(self.engs[eng])
        self.cnt[eng] += 1
        ins.then_inc(self.psem[eng], 1)
        tok = (("p", eng), self.cnt[eng])
        self._record(tok, reads, writes)
        return tok

    def dma(self, q, out, in_, reads=(), writes=(), **kw):
        deps = self._deps(reads, writes)
        i = self.dnext
        self.dnext = (self.dnext + 1) % len(self.dsem)
        if self.dcnt[i] > 0:
            deps.append((("d", i), self.dcnt[i]))
        self._wait(q, deps)
        ins = self.engs[q].dma_start(out=out, in_=in_, **kw)
        self.dcnt[i] += 16
        ins.then_inc(self.dsem[i], 16)
        tok = (("d", i), self.dcnt[i])
        self._record(tok, reads, writes)
        return tok

    def barrier(self):
        deps = []
        for e in self.engs:
            if self.cnt[e]:
                deps.append((("p", e), self.cnt[e]))
        for i in range(len(self.dsem)):
            if self.dcnt[i]:
                deps.append((("d", i), self.dcnt[i]))
        for e in self.engs:
            self._wait(e, deps)

    def mm(self, out, lhsT, rhs, start, stop, reads, writes):
        return self.op("pe", lambda e: e.matmul(out, lhsT, rhs, start=start, stop=stop),
                       reads=reads, writes=writes, same_engine=False)

    def tr(self, out, in_, ident, reads, writes):
        return self.op("pe", lambda e: e.transpose(out, in_, ident),
                       reads=reads, writes=writes, same_engine=False)

    def act(self, out, in_, func, reads, writes, **kw):
        return self.op("act", lambda e: e.activation(out=out, in_=in_, func=func, **kw),
                       reads=reads, writes=writes)

    def ts(self, eng, out, in0, s1, s2, op0, op1, reads, writes):
        if s2 is None:
            return self.op(eng, lambda e: e.tensor_scalar(out=out, in0=in0, scalar1=s1, scalar2=None, op0=op0),
                           reads=reads, writes=writes)
        return self.op(eng, lambda e: e.tensor_scalar(out=out, in0=in0, scalar1=s1, scalar2=s2, op0=op0, op1=op1),
                       reads=reads, writes=writes)

    def tt(self, eng, out, in0, in1, op, reads, writes):
        return self.op(eng, lambda e: e.tensor_tensor(out=out, in0=in0, in1=in1, op=op),
                       reads=reads, writes=writes)

    def stt(self, out, in0, scalar, in1, op0, op1, reads, writes):
        return self.op("dve", lambda e: e.scalar_tensor_tensor(out=out, in0=in0, scalar=scalar, in1=in1, op0=op0, op1=op1),
                       reads=reads, writes=writes)

    def cp(self, eng, out, in_, reads, writes):
        if eng == "act":
            return self.op("act", lambda e: e.copy(out=out, in_=in_), reads=reads, writes=writes)
        return self.op(eng, lambda e: e.tensor_copy(out=out, in_=in_), reads=reads, writes=writes)


def rows(t):
    return 128 if t < 16 else 16


def build_program(stop_after=99, dbg=None):
    nc = bass.Bass("TRN2", target_bir_lowering=False)

    def din(name, shape, dt=F32):
        return nc.dram_tensor(name, list(shape), dt, kind="ExternalInput").ap()

    def dout(name, shape):
        return nc.dram_tensor(name, list(shape), F32, kind="ExternalOutput").ap()

    xp = din("xp", [2048, 1024]); xs = din("xs", [16, 1024])
    pp = din("pp", [2048, 256]); psd = din("ps", [16, 256])
    sC = din("sC", [16, 4, 256, 256]); sn = din("sn", [16, 1024]); sm = din("sm", [16, 4])
    sconv = din("sconv", [16, 3, 2048])
    w_in = din("w_in", [1024, 10248]); g_pre = din("g_pre", [1024]); b_gates = din("b_gates", [8])
    conv_w = din("conv_w", [4, 2048]); conv_b = din("conv_b", [2048])
    g_head = din("g_head", [1024]); g_lnv = din("g_lnv", [1024]); b_lnv = din("b_lnv", [1024])
    w_s = din("w_s", [4, 128, 128]); b_s = din("b_s", [4, 128])
    w_pa = din("w_pa", [1024, 1024]); w_pb = din("w_pb", [1024, 1024]); w_out = din("w_out", [1024, 1024])
    g_post = din("g_post", [1024]); w_ple = din("w_ple", [256, 1024]); w_pg = din("w_pg", [1024, 1024])
    g_ple = din("g_ple", [1024])
    c_identb = din("c_identb", [128, 128], BF16); c_identf = din("c_identf", [128, 128])
    c_negmask = din("c_negmask", [128, 128], BF16); c_mask01 = din("c_mask01", [128, 128])
    c_sel8 = din("c_sel8", [8, 512]); c_small8 = din("c_small8", [8, 16])
    c_id16b = din("c_id16b", [128, 256])

    y_p = dout("y_p", [2048, 1024]); y_s = dout("y_s", [16, 1024])
    C_p = dout("C_p", [4, 256, 256]); n_p = dout("n_p", [4, 256]); m_p = dout("m_p", [4, 1])
    conv_p = dout("conv_p", [3, 2048])
    C_s = dout("C_s", [16, 4, 256, 256]); n_s = dout("n_s", [16, 1024]); m_s = dout("m_s", [16, 4])
    conv_s = dout("conv_s", [16, 3, 2048]); vn_s = dout("vn_s", [16, 1024])

    P = Prog(nc)
    sb = nc.alloc_sbuf_tensor
    _uid = [0]

    def sbt(name, shape, dt):
        _uid[0] += 1
        return nc.sbuf_tensor("%s_u%d" % (name, _uid[0]), shape, dt)

    HTf = sb("HT", [128, 8 * NT], BF16)
    QTf = sb("QT", [128, 8 * NT], BF16)
    KTf = sb("KT", [128, 8 * NT], BF16)
    HT = HTf[:, :].rearrange("p (k n) -> p k n", k=8)
    QT = QTf[:, :].rearrange("p (k n) -> p k n", k=8)
    KT = KTf[:, :].rearrange("p (k n) -> p k n", k=8)
    VV = sb("VV", [128, 17 * 1024], BF16)
    identb = sb("identb", [128, 128], BF16)
    identf = sb("identf", [128, 128], F32)
    negmask = sb("negmask", [128, 128], BF16)
    sel8 = sb("sel8", [8, 4, 128], F32)
    small8 = sb("small8", [8, 16], F32)
    id16b = sb("id16b", [128, 16, 16], F32)
    GPRE = sb("GPRE", [128, 8], F32)
    GHEAD = sb("GHEAD", [128, 8], F32)
    onesb = sb("onesb", [128, 2], BF16)
    zeros8 = sb("zeros8", [8, 128], F32)
    CW = sb("cw", [128, 16, 4], F32)
    CB = sb("cb", [128, 16], F32)
    esg = ExitStack()
    IG8 = esg.enter_context(sbt("IG8", [8, NT], F32))
    LF8 = esg.enter_context(sbt("LF8", [8, NT], F32))
    QKS = esg.enter_context(sbt("QKS", [16, 2048], F32))

    def Vt(t):
        return VV[:, t * 1024:(t + 1) * 1024]

    MT = VV[:, 0:8 * NT].rearrange("p (k n) -> p k n", k=8)

    PSB = [nc.alloc_psum_tensor("psb%d" % i, [128, 512], F32) for i in range(8)]

    def psk(i):
        return "ps%d" % i

    def psbf(i):
        return PSB[i][:, :].bitcast(BF16)

    P.dma("sp", identb[:, :], c_identb, writes=["const"])
    P.dma("sp", identf[:, :], c_identf, writes=["const"])
    P.dma("sp", negmask[:, :], c_negmask, writes=["const"])
    P.dma("sp", sel8[:, :, :], c_sel8.rearrange("k (h n) -> k h n", h=4), writes=["const"])
    P.dma("sp", small8[:, :], c_small8, writes=["const"])
    P.dma("sp", id16b[:, :, :], c_id16b.rearrange("p (a b) -> p a b", a=16), writes=["const"])
    P.op("dve", lambda e: e.memset(onesb[:, :], 1.0), writes=["const"])
    cw_pending = []
    def load_cw():
        with nc.allow_non_contiguous_dma(reason="tiny per-partition vectors"):
            for j in range(4):
                P.dma("sp", CW[:, :, j], conv_w[j].rearrange("(b p) -> p b", p=128), writes=["cw"])
            P.dma("sp", CB[:, :], conv_b.rearrange("(b p) -> p b", p=128), writes=["cw"])
            P.dma("sp", GPRE[:, :], g_pre.rearrange("(k p) -> p k", p=128), writes=["gvec"])
            P.dma("sp", GHEAD[:, :], g_head.rearrange("(k p) -> p k", p=128), writes=["gvec"])
            bgv = b_gates.rearrange("(a k o) -> a k o", a=2, o=1)
            P.dma("sp", small8[0:4, 10:11], bgv[0], writes=["bgv"])
            P.dma("sp", small8[4:8, 10:11], bgv[0], writes=["bgv"])
            P.dma("sp", small8[0:4, 11:12], bgv[1], writes=["bgv"])
            P.dma("sp", small8[4:8, 11:12], bgv[1], writes=["bgv"])
    P.op("dve", lambda e: e.memset(zeros8[:, :], 0.0), writes=["const"])
    HI, LO = small8[:, 0:1], small8[:, 1:2]
    MASKH = small8[:, 2:6]
    E8 = small8[:, 6:10]
    BGI, BGF = small8[:, 10:11], small8[:, 11:12]

    def rstd(es_tmp, ss, n, r, keys, extra=None):
        P.ts("dve", es_tmp, ss, 1.0 / n, EPS, ALU.mult, ALU.add, reads=keys, writes=keys)
        if extra is not None:
            P.tt("dve", es_tmp, es_tmp, extra, ALU.add, reads=keys, writes=keys)
        P.act(es_tmp, es_tmp, AF.Sqrt, reads=keys, writes=keys)
        P.op("dve", lambda e: e.reciprocal(out=r, in_=es_tmp), reads=keys, writes=keys)

    class WL:
        def __init__(self, es, nslab=2, ncol=256):
            self.stage = [es.enter_context(sbt("wstg%d" % i, [128, 8, ncol], F32)) for i in range(2)]
            self.slab = [es.enter_context(sbt("wslab%d" % i, [128, 8, ncol], BF16)) for i in range(nslab)]
            self.n = 0
            self.m = 0

        def load(self, w, c0, ncol, scale=None, kcn=8, dst=None, dstkey=None):
            i = self.n % 2
            self.n += 1
            sk = "wstg%d" % i
            P.dma("sp", self.stage[i][:, :kcn, :ncol],
                  w.rearrange("(k p) n -> p k n", p=128)[:, :, c0:c0 + ncol], writes=[sk])
            if dst is None:
                j = self.m % len(self.slab)
                self.m += 1
                dst = self.slab[j]
                dstkey = "wslab%d" % j
                dv = lambda kc: dst[:, kc, :ncol]
            else:
                dv = dst
            for kc in range(kcn):
                if scale is not None:
                    P.ts("pool", dv(kc), self.stage[i][:, kc, :ncol], scale[:, kc:kc + 1], 1.0, ALU.mult, ALU.mult,
                         reads=[sk, "const", "gvec"], writes=[dstkey])
                else:
                    P.cp("pool", dv(kc), self.stage[i][:, kc, :ncol], reads=[sk], writes=[dstkey])
            return dst, dstkey

    def proj_tok(ps_out, slab, skey, ncol, t, src, srckey, pskey, kcn=8, c0=0):
        r = rows(t)
        for kc in range(kcn):
            P.mm(ps_out, src[:, kc, t * 128:t * 128 + r], slab[:, kc, c0:c0 + ncol], kc == 0, kc == kcn - 1,
                 reads=[skey, srckey], writes=[pskey])

    def proj_feat(ps_out, slab, skey, c0, tok0, ntok, src, srckey, pskey, m=128):
        for kc in range(8):
            P.mm(ps_out, slab[:, kc, c0:c0 + m], src[:, kc, tok0:tok0 + ntok], kc == 0, kc == 7,
                 reads=[skey, srckey], writes=[pskey])

    def transpose_tile(srcf, srckey, t, dst, dstkey, bank, eng="act"):
        r = rows(t)
        pv = psbf(bank).rearrange("p (k n) -> p k n", k=8)
        for kc in range(8):
            P.tr(pv[:, kc, :r], srcf[:r, kc * 128:(kc + 1) * 128], identb[:r, :r],
                 reads=[srckey, "const"], writes=[psk(bank)])
        P.cp(eng, dst[:, :, t * 128:t * 128 + r], pv[:, :, :r], reads=[psk(bank)], writes=[dstkey])

    TOKG = [(0, 512), (512, 512), (1024, 512), (1536, 512), (2048, 16)]

    with ExitStack() as es:
        load_cw_after = 14
        XT = [es.enter_context(sbt("xt%d" % i, [128, 1024], F32)) for i in range(2)]
        XB = [es.enter_context(sbt("xb%d" % i, [128, 1024], BF16)) for i in range(2)]
        JK = [es.enter_context(sbt("jk%d" % i, [128, 1024], BF16)) for i in range(2)]
        st = [es.enter_context(sbt("st0%d" % i, [128, 4], F32)) for i in range(2)]
        for t in range(17):
            r = rows(t)
            q = t % 2
            xt = XT[q]
            xk = "xt%d" % q
            sk0 = "st0%d" % q
            src = xp[t * 128:(t + 1) * 128, :] if t < 16 else xs[:, :]
            P.dma("sp", xt[:r, :], src, writes=[xk])
            P.act(JK[q][:r, :], xt[:r, :], AF.Square, reads=[xk], writes=["jk%d" % q, sk0], accum_out=st[q][:r, 0:1])
            rstd(st[q][:r, 1:2], st[q][:r, 0:1], 1024.0, st[q][:r, 2:3], [sk0])
            P.ts("dve", XB[q][:r, :], xt[:r, :], st[q][:r, 2:3], None, ALU.mult, None, reads=[xk, sk0], writes=["xb%d" % q])
            transpose_tile(XB[q], "xb%d" % q, t, HT, "HT", q, eng="dve")
            if t == load_cw_after:
                load_cw()
    P.barrier()
    if stop_after <= 1:
        return nc

    with ExitStack() as es:
        wl = WL(es)
        QKP = [es.enter_context(sbt("qkpre%d" % i, [128, 2052], BF16)) for i in range(2)]
        DW = [es.enter_context(sbt("dw%d" % i, [128, 4, 128], BF16)) for i in range(2)]
        GT1 = es.enter_context(sbt("gt1", [8, 512], F32))
        T15 = es.enter_context(sbt("t15", [128, 256], F32))
        SQ = es.enter_context(sbt("sq", [16, 256], F32))
        SCV = es.enter_context(sbt("scv", [16, 3, 256], F32))
        CWB = es.enter_context(sbt("cwb", [16, 4, 256], F32))
        CBB = es.enter_context(sbt("cbb", [16, 256], F32))
        WG = es.enter_context(sbt("wg", [128, 8, 16], BF16))
        for i_ in range(2):
            P.op("dve", lambda e: e.memset(QKP[i_][:, 0:4], 0.0), writes=["qkpre%d_h" % i_])
        P.dma("sp", conv_s[:, 0:2, :], sconv[:, 1:3, :], writes=["conv_s01"])
        for sl in range(8):
            c0 = sl * 256
            slab, sk = wl.load(w_in, C_QK + c0, 256, scale=GPRE)
            proj_tok(PSB[6][:, 0:256], slab, sk, 256, 15, HT, "HT", psk(6))
            P.cp("act", T15[:, :], PSB[6][:, 0:256], reads=[psk(6)], writes=["t15"])
            P.dma("sp", conv_p[:, c0:c0 + 256], T15[125:128, :], reads=["t15"], writes=["conv_p"])
            proj_tok(PSB[7][:16, 0:256], slab, sk, 256, 16, HT, "HT", psk(7))
            P.cp("act", SQ[:, :], PSB[7][:16, 0:256], reads=[psk(7)], writes=["sq"])
            P.dma("sp", conv_s[:, 2, c0:c0 + 256], SQ[:, :], reads=["sq"], writes=["conv_s2"])
            P.dma("sp", SCV[:, :, :], sconv[:, :, c0:c0 + 256], writes=["scv"])
            P.dma("sp", CWB[:, :, :], conv_w[:, c0:c0 + 256].partition_broadcast(16), writes=["cwb"])
            P.dma("sp", CBB[:, :], conv_b[c0:c0 + 256].partition_broadcast(16), writes=["cwb"])
            P.tt("dve", SQ[:, :], SQ[:, :], CWB[:, 3, :], ALU.mult, reads=["sq", "cwb"], writes=["sq"])
            P.tt("dve", SQ[:, :], SQ[:, :], CBB[:, :], ALU.add, reads=["sq", "cwb"], writes=["sq"])
            for j in range(3):
                P.tt("dve", SCV[:, j, :], SCV[:, j, :], CWB[:, j, :], ALU.mult, reads=["scv", "cwb"], writes=["scv"])
                P.tt("dve", SQ[:, :], SQ[:, :], SCV[:, j, :], ALU.add, reads=["sq", "scv"], writes=["sq"])
            P.act(SQ[:, :], SQ[:, :], AF.Silu, reads=["sq"], writes=["sq"])
            P.ts("dve", QKS[:, c0:c0 + 256], SQ[:, :], 1.0 if sl < 4 else 0.0625, None, ALU.mult, None,
                 reads=["sq"], writes=["QKS"])
            for bb in range(2):
                b = sl * 2 + bb
                q_ = b % 2
                qk = "qkpre%d" % q_
                for tt_ in range(4):
                    bank = tt_ % 2
                    proj_feat(PSB[bank][:, :], slab, sk, bb * 128, tt_ * 512, 512, HT, "HT", psk(bank))
                    P.cp("dve", QKP[q_][:, 4 + tt_ * 512:4 + (tt_ + 1) * 512], PSB[bank][:, :],
                         reads=[psk(bank)], writes=[qk + "_%d" % tt_])
                for j in range(4):
                    P.ts("pool", DW[q_][:, j, :], identb[:, :], CW[:, b, j:j + 1], 1.0, ALU.mult, ALU.mult,
                         reads=["const", "cw"], writes=["dw%d" % q_])
                dstT, dk_ = (QT, "QT") if b < 8 else (KT, "KT")
                for tt_ in range(4):
                    bank2 = 2 + tt_ % 2
                    rk = [qk + "_%d" % tt_, (qk + "_%d" % (tt_ - 1)) if tt_ > 0 else (qk + "_h"), "dw%d" % q_]
                    for j in range(4):
                        P.mm(PSB[bank2][:, :], DW[q_][:, j, :], QKP[q_][:, 1 + j + tt_ * 512:1 + j + (tt_ + 1) * 512], j == 0, j == 3,
                             reads=rk, writes=[psk(bank2)])
                    P.act(dstT[:, b % 8, tt_ * 512:(tt_ + 1) * 512], PSB[bank2][:, :], AF.Silu, reads=[psk(bank2), "cw"], writes=[dk_],
                          bias=CB[:, b:b + 1])
        i = wl.n % 2
        wl.n += 1
        sk = "wstg%d" % i
        P.dma("sp", wl.stage[i][:, :, 0:8], w_in.rearrange("(k p) n -> p k n", p=128)[:, :, C_G:C_G + 8], writes=[sk])
        for kc in range(8):
            for (d0, s0) in ((0, 0), (4, 0), (8, 4), (12, 4)):
                P.ts("pool", WG[:, kc, d0:d0 + 4], wl.stage[i][:, kc, s0:s0 + 4], GPRE[:, kc:kc + 1], 1.0,
                     ALU.mult, ALU.mult, reads=[sk, "const", "gvec"], writes=["wg"])
        for (t0, n) in TOKG:
            proj_feat(PSB[0][:8, :n], WG, "wg", 0, t0, n, HT, "HT", psk(0), m=8)
            proj_feat(PSB[1][:8, :n], WG, "wg", 8, t0, n, HT, "HT", psk(1), m=8)
            P.ts("dve", IG8[:, t0:t0 + n], PSB[0][:8, :n], BGI, None, ALU.add, None, reads=[psk(0), "const", "bgv"], writes=["IG8"])
            P.ts("dve", GT1[:, :n], PSB[1][:8, :n], BGF, None, ALU.add, None, reads=[psk(1), "const", "bgv"], writes=["gt1"])
            P.act(GT1[:, :n], GT1[:, :n], AF.Exp, reads=["gt1"], writes=["gt1"], scale=-1.0)
            P.act(GT1[:, :n], GT1[:, :n], AF.Ln, reads=["gt1"], writes=["gt1"], bias=1.0)
            P.ts("dve", LF8[:, t0:t0 + n], GT1[:, :n], -1.0, None, ALU.mult, None, reads=["gt1"], writes=["LF8"])
        for sl in range(4):
            slab, sk = wl.load(w_in, C_V + sl * 256, 256, scale=GPRE)
            for t in range(17):
                r = rows(t)
                bank = 2 + t % 2
                proj_tok(PSB[bank][:r, 0:256], slab, sk, 256, t, HT, "HT", psk(bank))
                P.cp("act", Vt(t)[:r, sl * 256:(sl + 1) * 256], PSB[bank][:r, 0:256], reads=[psk(bank)], writes=["V%d" % t])
    P.barrier()
    if stop_after <= 2:
        return nc

    scr1 = nc.dram_tensor("scr_r1", [4, 2048], F32).ap()
    scr2 = nc.dram_tensor("scr_r2", [4, 2048], F32).ap()
    with ExitStack() as es:
        def T(name, shape, dt=F32):
            return es.enter_context(sbt(name, shape, dt))
        AT_all = T("AT_all", [128, 64]); EMT_all = T("EMT_all", [128, 64])
        NEGF = T("negf", [128, 128])
        P.dma("sp", NEGF[:, :], c_mask01, writes=["negf"])
        P.ts("dve", NEGF[:, :], NEGF[:, :], 30000.0, -30000.0, ALU.mult, ALU.add, reads=["negf"], writes=["negf"])
        with ExitStack() as esu:
            M8 = esu.enter_context(sbt("M8", [8, 2048], F32))
            FA = esu.enter_context(sbt("FA", [8, 2048], F32))
            P.op("dve", lambda e: e.tensor_tensor_scan(out=M8[:, :], data0=LF8[:, 0:2048], data1=IG8[:, 0:2048],
                                                       initial=0.0, op0=ALU.add, op1=ALU.max),
                 reads=["LF8", "IG8"], writes=["M8"])
            for c in range(16):
                cs = slice(c * 128, (c + 1) * 128)
                P.op("dve", lambda e: e.tensor_tensor_scan(out=FA[:, cs], data0=LF8[:, cs], data1=zeros8[:, :], initial=0.0,
                                                           op0=ALU.add, op1=ALU.add), reads=["LF8", "const"], writes=["FA"])
            P.tt("dve", IG8[:, 0:2048], IG8[:, 0:2048], FA[:, :], ALU.subtract, reads=["IG8", "FA"], writes=["IG8"])
            P.tt("dve", FA[:, :], FA[:, :], M8[:, :], ALU.subtract, reads=["FA", "M8"], writes=["FA"])
            for c in range(16):
                cs = slice(c * 128, (c + 1) * 128)
                P.mm(PSB[7][:, c * 4:c * 4 + 4], M8[:, cs], E8, True, True, reads=["M8", "const"], writes=[psk(7)])
                P.mm(PSB[6][:, c * 4:c * 4 + 4], IG8[:, cs], E8, True, True, reads=["IG8", "const"], writes=[psk(6)])
            P.act(EMT_all[:, :], PSB[7][:, 0:64], AF.Exp, reads=[psk(7)], writes=["emt"], scale=-1.0)
            P.cp("dve", AT_all[:, :], PSB[6][:, 0:64], reads=[psk(6)], writes=["at"])
            P.dma("sp", m_p, M8[0:4, 2047:2048], reads=["M8"], writes=["m_p"])
            for c in range(15, 0, -1):
                cs = slice(c * 128, (c + 1) * 128)
                P.ts("dve", M8[:, cs], FA[:, cs], M8[:, c * 128 - 1:c * 128], None, ALU.add, None, reads=["FA", "M8"], writes=["M8"])
            P.cp("dve", M8[:, 0:128], FA[:, 0:128], reads=["FA", "M8"], writes=["M8"])
            P.dma("sp", scr1, FA[0:4, :], reads=["FA"], writes=["scr1"])
            P.dma("sp", scr2, M8[0:4, :], reads=["M8"], writes=["scr2"])
            P.barrier()
        DT = [T("dT%d" % i, [128, 4, 128]) for i in range(2)]
        WBC = [T("wbc%d" % i, [128, 4, 128]) for i in range(2)]
        S = [T("S%d" % i, [128, 4, 128], BF16) for i in range(2)]
        QS = [T("QS%d" % i, [128, 8, 128], BF16) for i in range(2)]
        KW = [T("KW%d" % i, [128, 4, 256], BF16) for i in range(2)]
        CF = T("CF", [128, 2, 4, 256]); CBf = T("CBf", [128, 2, 4, 256], BF16)
        NF = T("NF", [128, 4, 2]); NB = T("NB", [128, 4, 2], BF16)
        SMT = T("SMT", [128, 32])
        JK2 = T("jk2", [128, 256], BF16)
        DCB = {(0, 0): 0, (0, 1): 1, (1, 0): 3, (1, 1): 0}

        def x_dma(c):
            q = c % 2
            cs = slice(c * 128, (c + 1) * 128)
            P.dma("sp", DT[q][:, :, :], scr1[:, cs].partition_broadcast(128), reads=["scr1"], writes=["dT%d" % q])
            P.dma("sp", WBC[q][:, :, :], scr2[:, cs].partition_broadcast(128), reads=["scr2"], writes=["wbc%d" % q])

        def x_pre(c):
            q = c % 2
            dk = "dT%d" % q
            P.tt("pool", DT[q][:, :, :], DT[q][:, :, :], NEGF[:, :].unsqueeze(1).to_broadcast([128, 4, 128]), ALU.add,
                 reads=[dk, "negf"], writes=[dk])
            for h in range(4):
                P.act(DT[q][:, h, :], DT[q][:, h, :], AF.Exp, reads=[dk, "at"], writes=[dk], bias=AT_all[:, c * 4 + h:c * 4 + h + 1])
            P.act(WBC[q][:, :, :], WBC[q][:, :, :], AF.Exp, reads=["wbc%d" % q], writes=["wbc%d" % q])

        def x_pe(c):
            cs = slice(c * 128, (c + 1) * 128)
            ST = PSB[2][:, :].rearrange("p (h n) -> p h n", h=4)
            for h in range(4):
                for j in range(2):
                    P.mm(ST[:, h, :], KT[:, 2 * h + j, cs], QT[:, 2 * h + j, cs], j == 0, j == 1,
                         reads=["KT", "QT"], writes=[psk(2)])
            KTR = psbf(6)
            for hj in range(8):
                P.tr(KTR[:, hj * 128:(hj + 1) * 128], KT[:, hj, cs], identb[:, :], reads=["KT", "const"], writes=[psk(6)])

        def x_post(c):
            q = c % 2
            cs = slice(c * 128, (c + 1) * 128)
            ST = PSB[2][:, :].rearrange("p (h n) -> p h n", h=4)
            KTR = psbf(6)
            P.stt(S[q][:, :, :], ST, 0.0625, DT[q][:, :, :], ALU.mult, ALU.mult, reads=[psk(2), "dT%d" % q], writes=["S%d" % q])
            for h in range(4):
                P.ts("dve", KW[q][:, h, :], KTR[:, h * 256:(h + 1) * 256], DT[q][:, h, 127:128], 0.0625, ALU.mult, ALU.mult,
                     reads=[psk(6), "dT%d" % q], writes=["KW%d" % q])
            if c > 0:
                for h in range(4):
                    P.tt("pool", QS[q][:, 2 * h:2 * h + 2, :], QT[:, 2 * h:2 * h + 2, cs],
                         WBC[q][:, h:h + 1, :].to_broadcast([128, 2, 128]), ALU.mult, reads=["QT", "wbc%d" % q], writes=["QS%d" % q])

        def y_dc(c):
            q = c % 2
            vk = "V%d" % c
            for j in range(2):
                for h in range(4):
                    bk, col = DCB[(j, h // 2)], (h % 2) * 256
                    P.mm(PSB[bk][:, col:col + 256], KW[q][:, h, j * 128:(j + 1) * 128], Vt(c)[:, h * 256:(h + 1) * 256],
                         True, True, reads=["KW%d" % q, vk], writes=[psk(bk)])
                    P.mm(PSB[7][:, 16 + h * 2 + j:17 + h * 2 + j], KW[q][:, h, j * 128:(j + 1) * 128], onesb[:, 0:1],
                         True, True, reads=["KW%d" % q, "const"], writes=[psk(7)])
                    if c == 0:
                        P.cp("dve", CF[:, j, h, :], PSB[bk][:, col:col + 256], reads=[psk(bk)], writes=["CF"])
                    else:
                        P.stt(CF[:, j, h, :], CF[:, j, h, :], WBC[q][:, h, 127:128], PSB[bk][:, col:col + 256],
                              ALU.mult, ALU.add, reads=["CF", "wbc%d" % q, psk(bk)], writes=["CF"])
            DN = PSB[7][:, 16:24].rearrange("p (h j) -> p h j", h=4)
            for h in range(4):
                if c == 0:
                    P.cp("dve", NF[:, h, :], DN[:, h, :], reads=[psk(7)], writes=["NF"])
                else:
                    P.stt(NF[:, h, :], NF[:, h, :], WBC[q][:, h, 127:128], DN[:, h, :], ALU.mult, ALU.add,
                          reads=["NF", "wbc%d" % q, psk(7)], writes=["NF"])

        def y_num(c):
            q = c % 2
            vk = "V%d" % c
            for h in range(4):
                nb_, col = 4 + h // 2, (h % 2) * 256
                P.mm(PSB[nb_][:, col:col + 256], S[q][:, h, :], Vt(c)[:, h * 256:(h + 1) * 256], True, c == 0,
                     reads=["S%d" % q, vk], writes=[psk(nb_)])
                if c > 0:
                    for j in range(2):
                        P.mm(PSB[nb_][:, col:col + 256], QS[q][:, 2 * h + j, :], CBf[:, j, h, :], False, j == 1,
                             reads=["QS%d" % q, "CBf"], writes=[psk(nb_)])
            for h in range(4):
                P.mm(PSB[7][:, 8 + h:9 + h], S[q][:, h, :], onesb[:, 0:1], True, c == 0, reads=["S%d" % q, "const"], writes=[psk(7)])
                if c > 0:
                    for j in range(2):
                        P.mm(PSB[7][:, 8 + h:9 + h], QS[q][:, 2 * h + j, :], NB[:, h, j:j + 1], False, j == 1,
                             reads=["QS%d" % q, "NB"], writes=[psk(7)])

        def y_h(c):
            vk = "V%d" % c
            EM = EMT_all[:, c * 4:c * 4 + 4]
            for h in range(4):
                nb_, col = 4 + h // 2, (h % 2) * 256
                P.act(JK2[:, :], PSB[nb_][:, col:col + 256], AF.Square, reads=[psk(nb_)], writes=["jk2", "smt_s"],
                      accum_out=SMT[:, 8 + h:9 + h])
            P.ts("dve", SMT[:, 12:16], PSB[7][:, 8:12], -1.0, None, ALU.mult, None, reads=[psk(7)], writes=["smt_t"])
            P.tt("dve", SMT[:, 4:8], PSB[7][:, 8:12], SMT[:, 12:16], ALU.max, reads=[psk(7), "smt_t"], writes=["smt_d"])
            P.tt("dve", SMT[:, 4:8], SMT[:, 4:8], EM, ALU.max, reads=["smt_d", "emt"], writes=["smt_d"])
            P.tt("dve", SMT[:, 4:8], SMT[:, 4:8], SMT[:, 4:8], ALU.mult, reads=["smt_d"], writes=["smt_d"])
            P.ts("dve", SMT[:, 12:16], SMT[:, 8:12], 1.0 / 256.0, None, ALU.mult, None, reads=["smt_s"], writes=["smt_t"])
            P.stt(SMT[:, 12:16], SMT[:, 4:8], EPS, SMT[:, 12:16], ALU.mult, ALU.add, reads=["smt_t", "smt_d"], writes=["smt_t"])
            P.act(SMT[:, 12:16], SMT[:, 12:16], AF.Sqrt, reads=["smt_t"], writes=["smt_t"])
            P.op("dve", lambda e: e.reciprocal(out=SMT[:, 16:20], in_=SMT[:, 12:16]), reads=["smt_t"], writes=["smt_c"])
            for h in range(4):
                nb_, col = 4 + h // 2, (h % 2) * 256
                P.ts("dve", Vt(c)[:, h * 256:(h + 1) * 256], PSB[nb_][:, col:col + 256], SMT[:, 16 + h:17 + h], None,
                     ALU.mult, None, reads=[psk(nb_), "smt_c"], writes=[vk])

        def y_cast(c):
            if c < 15:
                P.cp("act", CBf[:, :, :, :], CF[:, :, :, :], reads=["CF"], writes=["CBf"])
                P.cp("act", NB[:, :, :], NF[:, :, :], reads=["NF"], writes=["NB"])

        x_dma(0)
        x_dma(1)
        x_pre(0)
        x_pe(0)
        x_post(0)
        for c in range(16):
            y_dc(c)
            if c + 1 < 16:
                x_pre(c + 1)
                x_pe(c + 1)
            if c + 1 < 16:
                x_post(c + 1)
            if c + 2 < 16:
                x_dma(c + 2)
            y_num(c)
            y_h(c)
            y_cast(c)
        for j in range(2):
            P.dma("sp", C_p[:, j * 128:(j + 1) * 128, :].rearrange("h p v -> p h v"), CF[:, j, :, :], reads=["CF"], writes=["C_p"])
        with nc.allow_non_contiguous_dma(reason="tiny state vector"):
            P.dma("sp", n_p.rearrange("h (j p) -> p h j", p=128), NF[:, :, :], reads=["NF"], writes=["n_p"])
    P.barrier()
    if stop_after <= 3:
        return nc

    with ExitStack() as es:
        def T(name, shape, dt=F32):
            return es.enter_context(sbt(name, shape, dt))
        CS = [T("cs%d" % i, [128, 4, 2, 256]) for i in range(2)]
        KDd = T("kdd", [16, 1024])
        G = T("gs8", [8, 6, 16])
        SM = T("smts", [16, 40])
        WPB = T("wpb", [128, 4, 16])
        SN = T("snn", [16, 1024]); TMPF = T("tmpf", [16, 1024]); KD = T("kd", [16, 1024])
        KDs = [T("kds%d" % i, [16, 1024], BF16) for i in range(2)]
        QTs = T("qts", [128, 8, 16]); QMs = [T("qms%d" % i, [128, 8, 16], BF16) for i in range(2)]
        CSB = [T("csb%d" % i, [128, 4, 2, 256], BF16) for i in range(2)]
        JK3 = T("jk3", [16, 256], BF16)
        sc_ = slice(2048, 2064)
        with nc.allow_non_contiguous_dma(reason="tiny state vector"):
            P.dma("sp", G[0:4, 0, :], sm.rearrange("s h -> h s"), writes=["gs"])
            P.dma("sp", G[4:8, 0, :], sm.rearrange("s h -> h s"), writes=["gs"])
        P.dma("sp", SN[:, :], sn, writes=["snn"])
        P.tt("dve", G[:, 1, :], LF8[:, sc_], G[:, 0, :], ALU.add, reads=["LF8", "gs"], writes=["gs"])
        P.tt("dve", G[:, 2, :], G[:, 1, :], IG8[:, sc_], ALU.max, reads=["IG8", "gs"], writes=["gs"])
        P.tt("dve", G[:, 3, :], IG8[:, sc_], G[:, 2, :], ALU.subtract, reads=["IG8", "gs"], writes=["gs"])
        P.act(G[:, 3, :], G[:, 3, :], AF.Exp, reads=["gs"], writes=["gs"])
        P.tt("dve", G[:, 4, :], G[:, 1, :], G[:, 2, :], ALU.subtract, reads=["gs"], writes=["gs"])
        P.act(G[:, 4, :], G[:, 4, :], AF.Exp, reads=["gs"], writes=["gs"])
        for i_, gi in enumerate((2, 3, 4)):
            P.mm(PSB[7][:16, 4 * i_:4 * i_ + 4], G[:, gi, :], E8, True, True, reads=["gs", "const"], writes=[psk(7)])
        P.cp("dve", SM[:, 0:12], PSB[7][:16, 0:12], reads=[psk(7)], writes=["sm"])
        P.dma("sp", m_s, SM[:, 0:4], reads=["sm"], writes=["m_s"])
        for h in range(4):
            P.mm(PSB[7][:, 32 + h * 16:48 + h * 16], sel8[:, h, :], G[:, 4, :], True, True, reads=["gs", "const"], writes=[psk(7)])
        P.cp("dve", WPB[:, :, :], PSB[7][:, 32:96].rearrange("p (h s) -> p h s", h=4), reads=[psk(7)], writes=["wpb"])
        Qs, Ks = QKS[:, 0:1024], QKS[:, 1024:2048]
        V16 = Vt(16)
        P.tt("dve", TMPF[:, :], Qs, Ks, ALU.mult, reads=["QKS"], writes=["tmpf"])
        P.op("dve", lambda e: e.tensor_reduce(out=SM[:, 12:16], in_=TMPF[:, :].rearrange("p (h d) -> p h d", h=4),
                                              axis=AX.X, op=ALU.add), reads=["tmpf"], writes=["sm"])
        P.tt("dve", TMPF[:, :], Qs, SN[:, :], ALU.mult, reads=["QKS", "snn"], writes=["tmpf"])
        P.op("dve", lambda e: e.tensor_reduce(out=SM[:, 16:20], in_=TMPF[:, :].rearrange("p (h d) -> p h d", h=4),
                                              axis=AX.X, op=ALU.add), reads=["tmpf"], writes=["sm"])
        P.tt("dve", SM[:, 12:16], SM[:, 12:16], SM[:, 4:8], ALU.mult, reads=["sm"], writes=["sm"])
        P.tt("dve", SM[:, 20:24], SM[:, 16:20], SM[:, 8:12], ALU.mult, reads=["sm"], writes=["sm"])
        P.tt("dve", SM[:, 20:24], SM[:, 20:24], SM[:, 12:16], ALU.add, reads=["sm"], writes=["sm"])
        P.ts("dve", SM[:, 28:32], SM[:, 20:24], -1.0, None, ALU.mult, None, reads=["sm"], writes=["sm"])
        P.tt("dve", SM[:, 20:24], SM[:, 20:24], SM[:, 28:32], ALU.max, reads=["sm"], writes=["sm"])
        P.act(SM[:, 28:32], SM[:, 0:4], AF.Exp, reads=["sm"], writes=["sm"], scale=-1.0)
        P.tt("dve", SM[:, 20:24], SM[:, 20:24], SM[:, 28:32], ALU.max, reads=["sm"], writes=["sm"])
        for h in range(4):
            hs = slice(h * 256, (h + 1) * 256)
            P.ts("dve", KD[:, hs], Ks[:, hs], SM[:, 4 + h:5 + h], None, ALU.mult, None, reads=["QKS", "sm"], writes=["kd"])
            P.stt(SN[:, hs], SN[:, hs], SM[:, 8 + h:9 + h], KD[:, hs], ALU.mult, ALU.add, reads=["snn", "sm", "kd"], writes=["snn"])
        P.dma("sp", n_s, SN[:, :], reads=["snn"], writes=["n_s"])
        P.cp("dve", KDd[:, :].rearrange("p (h j q) -> p h j q", h=4, j=2), KD[:, :].rearrange("p (h q j) -> p h j q", h=4, j=2),
             reads=["kd"], writes=["kdd"])
        P.cp("dve", TMPF[:, :].rearrange("p (h j q) -> p h j q", h=4, j=2), Qs.rearrange("p (h q j) -> p h j q", h=4, j=2),
             reads=["QKS", "tmpf"], writes=["tmpf"])
        QTP = PSB[6][:, 0:128].rearrange("p (k s) -> p k s", k=8)
        for hj in range(8):
            P.tr(QTP[:, hj, :], TMPF[:, hj * 128:(hj + 1) * 128], identf[:16, :16], reads=["tmpf", "const"], writes=[psk(6)])
        P.cp("dve", QTs[:, :, :], QTP, reads=[psk(6)], writes=["qts"])
        def cs_load(s_):
            b = s_ % 2
            P.dma("sp", CS[b][:, :, :, :].rearrange("p h j v -> p h (j v)"), sC[s_].rearrange("h (p j) v -> p h (j v)", j=2),
                  writes=["cs%d" % b])

        def cast(s_):
            b = s_ % 2
            P.cp("act", CSB[b][:, :, :, :], CS[b][:, :, :, :], reads=["cs%d" % b], writes=["csb%d" % b])

        def prep(s_):
            q = s_ % 2
            P.tt("dve", QMs[q][:, :, :], QTs[:, :, :], id16b[:, s_:s_ + 1, :].to_broadcast([128, 8, 16]), ALU.mult,
                 reads=["qts", "const"], writes=["qms%d" % q])
            P.ts("dve", KDs[q][:, :], KDd[:, :], identf[:16, s_:s_ + 1], None,
                 ALU.mult, None, reads=["kdd", "const"], writes=["kds%d" % q])
        cs_load(0)
        cs_load(1)
        prep(0)
        cast(0)
        for s_ in range(16):
            b = s_ % 2
            q = s_ % 2
            for h in range(4):
                for j in range(2):
                    P.mm(PSB[h][:16, 0:256], QMs[q][:, 2 * h + j, :], CSB[b][:, h, j, :], s_ == 0 and j == 0, s_ == 15 and j == 1,
                         reads=["qms%d" % q, "csb%d" % b], writes=[psk(h)])
            for j in range(2):
                for h in range(4):
                    bk, col = 4 + 2 * j + h // 2, (h % 2) * 256
                    P.mm(PSB[bk][:, col:col + 256], KDs[q][:, h * 256 + j * 128:h * 256 + (j + 1) * 128],
                         V16[:16, h * 256:(h + 1) * 256], True, True, reads=["kds%d" % q, "V16"], writes=[psk(bk)])
            if s_ + 1 < 16:
                cast(s_ + 1)
                prep(s_ + 1)
            for j in range(2):
                for h in range(4):
                    bk, col = 4 + 2 * j + h // 2, (h % 2) * 256
                    P.stt(CS[b][:, h, j, :], CS[b][:, h, j, :], WPB[:, h, s_:s_ + 1], PSB[bk][:, col:col + 256],
                          ALU.mult, ALU.add, reads=["cs%d" % b, "wpb", psk(bk)], writes=["cs%d" % b])
            P.dma("act", C_s[s_].rearrange("h (p j) v -> p h (j v)", j=2), CS[b][:, :, :, :].rearrange("p h j v -> p h (j v)"),
                  reads=["cs%d" % b], writes=["C_s"])
            if s_ + 2 < 16:
                cs_load(s_ + 2)
        for h in range(4):
            hs = slice(h * 256, (h + 1) * 256)
            P.ts("dve", TMPF[:, hs], V16[:16, hs], SM[:, 12 + h:13 + h], None, ALU.mult, None, reads=["V16", "sm"], writes=["tmpf"])
            P.stt(TMPF[:, hs], PSB[h][:16, 0:256], SM[:, 8 + h:9 + h], TMPF[:, hs], ALU.mult, ALU.add,
                  reads=[psk(h), "sm", "tmpf"], writes=["tmpf"])
            P.act(JK3[:, :], TMPF[:, hs], AF.Square, reads=["tmpf"], writes=["jk3", "sm2"], accum_out=SM[:, 24 + h:25 + h])
        P.tt("dve", SM[:, 28:32], SM[:, 20:24], SM[:, 20:24], ALU.mult, reads=["sm", "sm2"], writes=["sm"])
        P.ts("dve", SM[:, 28:32], SM[:, 28:32], EPS, None, ALU.mult, None, reads=["sm"], writes=["sm"])
        P.ts("dve", SM[:, 24:28], SM[:, 24:28], 1.0 / 256.0, None, ALU.mult, None, reads=["sm", "sm2"], writes=["sm"])
        P.tt("dve", SM[:, 28:32], SM[:, 28:32], SM[:, 24:28], ALU.add, reads=["sm"], writes=["sm"])
        P.act(SM[:, 28:32], SM[:, 28:32], AF.Sqrt, reads=["sm"], writes=["sm"])
        P.op("dve", lambda e: e.reciprocal(out=SM[:, 32:36], in_=SM[:, 28:32]), reads=["sm"], writes=["sm"])
        for h in range(4):
            hs = slice(h * 256, (h + 1) * 256)
            P.ts("dve", V16[:16, hs], TMPF[:, hs], SM[:, 32 + h:33 + h], None, ALU.mult, None, reads=["tmpf", "sm"], writes=["V16"])
    P.barrier()
    esg.close()
    if stop_after <= 4:
        return nc

    with ExitStack() as es:
        wl = WL(es)
        TMP = es.enter_context(sbt("tmpE", [128, 256], F32))
        for (cbase, fn) in ((C_O, AF.Sigmoid), (C_ZA, AF.Silu)):
            for sl in range(4):
                slab, sk = wl.load(w_in, cbase + sl * 256, 256, scale=GPRE)
                for t in range(17):
                    r = rows(t)
                    bank = t % 2
                    vk = "V%d" % t
                    proj_tok(PSB[bank][:r, 0:256], slab, sk, 256, t, HT, "HT", psk(bank))
                    P.act(TMP[:r, :], PSB[bank][:r, 0:256], fn, reads=[psk(bank)], writes=["tmpE"])
                    P.tt("dve", Vt(t)[:r, sl * 256:(sl + 1) * 256], Vt(t)[:r, sl * 256:(sl + 1) * 256], TMP[:r, :], ALU.mult,
                         reads=[vk, "tmpE"], writes=[vk])
        for t in range(17):
            transpose_tile(Vt(t), "V%d" % t, t, QT, "QT", 2 + t % 2)
    P.barrier()
    if stop_after <= 5:
        return nc

    with ExitStack() as es:
        wl = WL(es)
        def T(name, shape, dt=F32):
            return es.enter_context(sbt(name, shape, dt))
        WST = T("wst", [128, 4, 128], BF16); WSL = T("wsl", [128, 128]); M01 = T("m01", [128, 128])
        BST = T("bst", [128, 4]); W00 = T("w00", [16, 8]); RS = T("rs", [128, 4])
        GL = T("gl", [128, 1024]); BL = T("bl", [128, 1024]); RB = T("rb", [128, 1024])
        LNS = T("lns", [128, 17, 4, 2]); MV = [T("mv%d" % i, [128, 8]) for i in range(2)]
        TMPB = T("tmpb", [128, 256], BF16)
        TMP = [T("tmpG%d" % i, [128, 512]) for i in range(2)]
        P.dma("sp", GL[:, :], g_lnv.partition_broadcast(128), writes=["gl"])
        P.dma("sp", BL[:, :], b_lnv.partition_broadcast(128), writes=["gl"])
        P.dma("sp", M01[:, :], c_mask01, writes=["m01"])
        with nc.allow_non_contiguous_dma(reason="tiny vectors"):
            P.dma("sp", BST[:, :], b_s.rearrange("g t -> t g"), writes=["bst"])
            P.dma("sp", W00[:, 0:4], w_s[:, 0, 0].partition_broadcast(16), writes=["w00"])
            P.dma("sp", W00[:, 4:8], b_s[:, 0].partition_broadcast(16), writes=["w00"])
        for g in range(4):
            P.dma("sp", WSL[:, :], w_s[g], writes=["wsl"])
            P.tr(PSB[7][:, 0:128], WSL[:, :], identf[:, :], reads=["wsl", "const"], writes=[psk(7)])
            P.tt("dve", WST[:, g, :], PSB[7][:, 0:128], M01[:, :], ALU.mult, reads=[psk(7), "m01"], writes=["wst"])
        for g in range(4):
            P.mm(PSB[7][:, 256 + g:257 + g], WST[:, g, :], onesb[:, 0:1], True, True, reads=["wst", "const"], writes=[psk(7)])
        P.cp("dve", RS[:, :], PSB[7][:, 256:260], reads=[psk(7)], writes=["rs"])
        for g in range(4):
            gs = slice(g * 256, (g + 1) * 256)
            P.ts("dve", RB[:, gs], BL[:, gs], RS[:, g:g + 1], BST[:, g:g + 1], ALU.mult, ALU.add, reads=["gl", "rs", "bst"], writes=["rb"])
        for sl in range(4):
            slab, sk = wl.load(w_in, C_VB + sl * 256, 256, scale=GPRE)
            for t in range(17):
                r = rows(t)
                bank = t % 2
                vk = "V%d" % t
                proj_tok(PSB[bank][:r, 0:256], slab, sk, 256, t, HT, "HT", psk(bank))
                P.act(Vt(t)[:r, sl * 256:(sl + 1) * 256], PSB[bank][:r, 0:256], AF.Copy, reads=[psk(bank)],
                      writes=[vk, "lns%d" % t], accum_out=LNS[:r, t, sl, 0:1])
                P.act(TMPB[:r, :], PSB[bank][:r, 0:256], AF.Square, reads=[psk(bank)],
                      writes=["tmpb", "lns%d" % t], accum_out=LNS[:r, t, sl, 1:2])

        def pre(t):
            r = rows(t); q = t % 2
            mv = MV[q]; mk = "mv%d" % q
            vk = "V%d" % t
            P.op("dve", lambda e: e.tensor_reduce(out=mv[:r, 0:2], in_=LNS[:r, t, :, :].rearrange("p a b -> p b a"),
                                                  axis=AX.X, op=ALU.add), reads=["lns%d" % t], writes=[mk])
            P.ts("dve", mv[:r, 0:2], mv[:r, 0:2], 1.0 / 1024.0, None, ALU.mult, None, reads=[mk], writes=[mk])
            P.tt("dve", mv[:r, 2:3], mv[:r, 0:1], mv[:r, 0:1], ALU.mult, reads=[mk], writes=[mk])
            P.tt("dve", mv[:r, 1:2], mv[:r, 1:2], mv[:r, 2:3], ALU.subtract, reads=[mk], writes=[mk])
            P.ts("dve", mv[:r, 2:3], mv[:r, 1:2], EPS, None, ALU.add, None, reads=[mk], writes=[mk])
            P.act(mv[:r, 2:3], mv[:r, 2:3], AF.Sqrt, reads=[mk], writes=[mk])
            P.op("dve", lambda e: e.reciprocal(out=mv[:r, 3:4], in_=mv[:r, 2:3]), reads=[mk], writes=[mk])
            P.stt(mv[:r, 4:5], mv[:r, 0:1], -1.0, mv[:r, 3:4], ALU.mult, ALU.mult, reads=[mk], writes=[mk])
            P.act(Vt(t)[:r, :], Vt(t)[:r, :], AF.Identity, reads=[vk, mk], writes=[vk], scale=mv[:r, 3:4], bias=mv[:r, 4:5])

        def mix(t):
            vk = "V%d" % t
            if t == 16:
                VNS = TMP[0][:16, :].rearrange("p (a b) -> p a b", a=2)
                return
            for g in range(4):
                bk, col = 2 + g // 2, (g % 2) * 256
                P.mm(PSB[bk][:, col:col + 256], WST[:, g, :], Vt(t)[:, g * 256:(g + 1) * 256], True, True,
                     reads=["wst", vk], writes=[psk(bk)])

        def mix_evac(t):
            vk = "V%d" % t
            for hf in range(2):
                hs = slice(hf * 512, (hf + 1) * 512)
                tm = TMP[hf]; tk_ = "tmpG%d" % hf
                P.tt("dve", tm[:, :], PSB[2 + hf][:, :], GL[:, hs], ALU.mult, reads=[psk(2 + hf), "gl"], writes=[tk_])
                P.tt("dve", Vt(t)[:, hs], tm[:, :], RB[:, hs], ALU.add, reads=[tk_, "rb"], writes=[vk])

        def sample_mix():
            vk = "V16"
            VN = [TMP[0][:16, :], TMP[1][:16, :]]
            for hf in range(2):
                hs = slice(hf * 512, (hf + 1) * 512)
                tk_ = "tmpG%d" % hf
                P.tt("dve", VN[hf], Vt(16)[:16, hs], GL[:16, hs], ALU.mult, reads=[vk, "gl"], writes=[tk_])
                P.tt("dve", VN[hf], VN[hf], BL[:16, hs], ALU.add, reads=[tk_, "gl"], writes=[tk_])
                P.dma("sp", vn_s[:, hs], VN[hf], reads=[tk_], writes=["vn_s"])
                V3 = Vt(16)[:16, hs].rearrange("p (g d) -> p g d", g=2)
                VN3 = VN[hf].rearrange("p (g d) -> p g d", g=2)
                P.tt("dve", VN3, VN3, W00[:, 2 * hf:2 * hf + 2].unsqueeze(2).to_broadcast([16, 2, 256]), ALU.mult,
                     reads=[tk_, "w00"], writes=[tk_])
                P.tt("dve", V3, VN3, W00[:, 4 + 2 * hf:6 + 2 * hf].unsqueeze(2).to_broadcast([16, 2, 256]), ALU.add,
                     reads=[tk_, "w00"], writes=[vk])

        pre(0)
        for sl in range(4):
            slab, sk = wl.load(w_in, C_U + sl * 256, 256, scale=GPRE)
            for t in range(17):
                r = rows(t)
                bank = t % 2
                vk = "V%d" % t
                if sl == 0:
                    if t + 1 < 17:
                        pre(t + 1)
                    if t < 16:
                        mix(t)
                proj_tok(PSB[bank][:r, 0:256], slab, sk, 256, t, HT, "HT", psk(bank))
                if sl == 0:
                    if t < 16:
                        mix_evac(t)
                    else:
                        sample_mix()
                Y = Vt(t)[:r, sl * 256:(sl + 1) * 256]
                P.tt("dve", Y, PSB[bank][:r, 0:256], Y, ALU.mult, reads=[psk(bank), vk], writes=[vk])
        for sl in range(4):
            slab, sk = wl.load(w_in, C_ZB + sl * 256, 256, scale=GPRE)
            for t in range(17):
                r = rows(t)
                bank = t % 2
                vk = "V%d" % t
                tm = TMP[t % 2]; tk_ = "tmpG%d" % (t % 2)
                proj_tok(PSB[bank][:r, 0:256], slab, sk, 256, t, HT, "HT", psk(bank))
                if sl == 3 and t >= 1:
                    transpose_tile(Vt(t - 1), "V%d" % (t - 1), t - 1, KT, "KT", 4 + (t - 1) % 2)
                Y = Vt(t)[:r, sl * 256:(sl + 1) * 256]
                P.act(tm[:r, 0:256], PSB[bank][:r, 0:256], AF.Silu, reads=[psk(bank)], writes=[tk_])
                P.tt("dve", Y, Y, tm[:r, 0:256], ALU.mult, reads=[vk, tk_], writes=[vk])
        transpose_tile(Vt(16), "V16", 16, KT, "KT", 4)
    P.barrier()
    if stop_after <= 6:
        return nc

    esw = ExitStack()
    WOf = esw.enter_context(sbt("wo", [128, 8, 1024], BF16))
    WPGf = esw.enter_context(sbt("wpg", [128, 8, 1024], BF16))
    WPLf = esw.enter_context(sbt("wpl", [128, 2, 1024], BF16))
    with ExitStack() as es:
        wl = WL(es, nslab=3)
        SG = [es.enter_context(sbt("sg%d" % i, [128, 512], F32)) for i in range(2)]
        TM2 = es.enter_context(sbt("tm2", [128, 512], F32))
        fin_loads = []
        for sl in range(4):
            fin_loads.append((w_out, sl, 8, WOf, "WO"))
        for sl in range(4):
            fin_loads.append((w_pg, sl, 8, WPGf, "WPG"))
        for sl in range(4):
            fin_loads.append((w_ple, sl, 2, WPLf, "WPL"))
        it = 0
        for br, (wp_, cg, srcT, srck, scl) in enumerate(((w_pa, C_GA, QT, "QT", GHEAD), (w_pb, C_GB, KT, "KT", None))):
            for sl in range(4):
                slab_p, skp = wl.load(wp_, sl * 256, 256, scale=scl)
                slab_g, skg = wl.load(w_in, cg + sl * 256, 256, scale=GPRE)
                for bb in range(2):
                    b = sl * 2 + bb
                    for gi, (t0, n) in enumerate(TOKG):
                        it += 1
                        ba, bg = (it % 2) * 2, (it % 2) * 2 + 1
                        sg = SG[it % 2]
                        sgk = "sg%d" % (it % 2)
                        proj_feat(PSB[bg][:, :n], slab_g, skg, bb * 128, t0, n, HT, "HT", psk(bg))
                        proj_feat(PSB[ba][:, :n], slab_p, skp, bb * 128, t0, n, srcT, srck, psk(ba))
                        P.act(sg[:, :n], PSB[bg][:, :n], AF.Sigmoid, reads=[psk(bg)], writes=[sgk])
                        if br == 0:
                            P.tt("dve", MT[:, b, t0:t0 + n], PSB[ba][:, :n], sg[:, :n], ALU.mult, reads=[psk(ba), sgk], writes=["MT"])
                        else:
                            P.tt("dve", TM2[:, :n], PSB[ba][:, :n], sg[:, :n], ALU.mult, reads=[psk(ba), sgk], writes=["tm2"])
                            P.tt("dve", MT[:, b, t0:t0 + n], MT[:, b, t0:t0 + n], TM2[:, :n], ALU.add, reads=["MT", "tm2"], writes=["MT"])
                for _ in range(2 if fin_loads and len(fin_loads) > 4 else 1):
                    if fin_loads:
                        (w_, sl_, kcn_, dstw, dkey) = fin_loads.pop(0)
                        wl.load(w_, sl_ * 256, 256, kcn=kcn_, dst=(lambda kc, dstw=dstw, sl_=sl_: dstw[:, kc, sl_ * 256:(sl_ + 1) * 256]),
                                dstkey=dkey)
        while fin_loads:
            (w_, sl_, kcn_, dstw, dkey) = fin_loads.pop(0)
            wl.load(w_, sl_ * 256, 256, kcn=kcn_, dst=(lambda kc, dstw=dstw, sl_=sl_: dstw[:, kc, sl_ * 256:(sl_ + 1) * 256]), dstkey=dkey)
    P.barrier()
    if stop_after <= 7:
        return nc

    with ExitStack() as es:
        WO, WPG, WPL = WOf, WPGf, WPLf
        def T(name, shape, dt=F32):
            return es.enter_context(sbt(name, shape, dt))
        def carve(flat, off_elems, n_f32):
            return flat[:, off_elems:off_elems + 2 * n_f32].bitcast(F32)
        GPO = carve(HTf, 0, 1024); GPL = carve(HTf, 2048, 1024)
        XT = [carve(HTf, 4096, 1024), carve(HTf, 6144, 1024)]
        X1 = [carve(HTf, 8192, 1024), carve(HTf, 10240, 1024)]
        YO = [carve(QTf, 0, 1024), carve(QTf, 2048, 1024)]
        SIG = carve(QTf, 4096, 1024)
        PTK = [carve(QTf, 6144, 256), carve(QTf, 6656, 256)]
        X1B = [QTf[:, 8192:9216], QTf[:, 9216:10240]]
        X1T = [QTf[:, 10240:11264].rearrange("p (k n) -> p k n", k=8), QTf[:, 11264:12288].rearrange("p (k n) -> p k n", k=8)]
        PTB = QTf[:, 12288:12544]
        PTT = [QTf[:, 12544:12800].rearrange("p (k n) -> p k n", k=2), QTf[:, 12800:13056].rearrange("p (k n) -> p k n", k=2)]
        JK = KTf[:, 0:1024]
        sf = T("sf", [128, 2, 8])
        P.dma("sp", GPO[:, :], g_post.partition_broadcast(128), writes=["gpo"])
        P.dma("sp", GPL[:, :], g_ple.partition_broadcast(128), writes=["gpo"])

        def load(i):
            r = rows(i); q = i % 2
            P.dma("sp", XT[q][:r, :], xp[i * 128:(i + 1) * 128, :] if i < 16 else xs[:, :], writes=["xtf%d" % q])
            P.dma("sp", PTK[q][:r, :], pp[i * 128:(i + 1) * 128, :] if i < 16 else psd[:, :], writes=["ptk%d" % q])

        def stage_A(i):
            r = rows(i); q = i % 2
            tk = slice(i * 128, i * 128 + r)
            for half in range(2):
                for kc in range(8):
                    P.mm(PSB[half][:r, :], MT[:, kc, tk], WO[:, kc, half * 512:(half + 1) * 512], kc == 0, kc == 7,
                         reads=["MT", "WO"], writes=[psk(half)])

        def stage_B(i):
            r = rows(i); q = i % 2
            s_ = sf[:, q, :]
            sk_ = "sf%d" % q
            for half in range(2):
                P.act(JK[:r, half * 512:(half + 1) * 512], PSB[half][:r, :], AF.Square, reads=[psk(half)], writes=["jkf", sk_],
                      accum_out=s_[:r, half:half + 1])
            P.tt("dve", s_[:r, 0:1], s_[:r, 0:1], s_[:r, 1:2], ALU.add, reads=[sk_], writes=[sk_])
            rstd(s_[:r, 2:3], s_[:r, 0:1], 1024.0, s_[:r, 3:4], [sk_])
            for half in range(2):
                hs = slice(half * 512, (half + 1) * 512)
                P.stt(X1[q][:r, hs], PSB[half][:r, :], s_[:r, 3:4], GPO[:r, hs], ALU.mult, ALU.mult,
                      reads=[psk(half), sk_, "gpo"], writes=["x1%d" % q])
            P.tt("dve", X1[q][:r, :], X1[q][:r, :], XT[q][:r, :], ALU.add, reads=["x1%d" % q, "xtf%d" % q], writes=["x1%d" % q])
            P.cp("pool", X1B[q][:r, :], X1[q][:r, :], reads=["x1%d" % q], writes=["x1b%d" % q])
            P.cp("pool", PTB[:r, :], PTK[q][:r, :], reads=["ptk%d" % q], writes=["ptb"])

        def stage_C(i):
            r = rows(i); q = i % 2
            pv2 = psbf(3).rearrange("p (k n) -> p k n", k=8)
            for kc in range(2):
                P.tr(pv2[:, kc, :r], PTB[:r, kc * 128:(kc + 1) * 128], identb[:r, :r], reads=["ptb", "const"], writes=[psk(3)])
            P.cp("dve", PTT[q][:, :, :r], pv2[:, 0:2, :r], reads=[psk(3)], writes=["ptt%d" % q])
            pv = psbf(2).rearrange("p (k n) -> p k n", k=8)
            for kc in range(8):
                P.tr(pv[:, kc, :r], X1B[q][:r, kc * 128:(kc + 1) * 128], identb[:r, :r], reads=["x1b%d" % q, "const"], writes=[psk(2)])
            P.cp("act", X1T[q][:, :, :r], pv[:, :, :r], reads=[psk(2)], writes=["x1t%d" % q])
            for half in range(2):
                for kc in range(2):
                    P.mm(PSB[6 + half][:r, :], PTT[q][:, kc, :r], WPL[:, kc, half * 512:(half + 1) * 512], kc == 0, kc == 1,
                         reads=["ptt%d" % q, "WPL"], writes=[psk(6 + half)])
            for half in range(2):
                for kc in range(8):
                    P.mm(PSB[4 + half][:r, :], X1T[q][:, kc, :r], WPG[:, kc, half * 512:(half + 1) * 512], kc == 0, kc == 7,
                         reads=["x1t%d" % q, "WPG"], writes=[psk(4 + half)])

        def stage_D(i):
            r = rows(i); q = i % 2
            s_ = sf[:, q, :]
            sk_ = "sg%d" % q
            for half in range(2):
                hs = slice(half * 512, (half + 1) * 512)
                P.act(JK[:r, hs], PSB[6 + half][:r, :], AF.Square, reads=[psk(6 + half)], writes=["jkf", sk_],
                      accum_out=s_[:r, 4 + half:5 + half])
            P.tt("dve", s_[:r, 4:5], s_[:r, 4:5], s_[:r, 5:6], ALU.add, reads=[sk_], writes=[sk_])
            rstd(s_[:r, 6:7], s_[:r, 4:5], 1024.0, s_[:r, 7:8], [sk_])
            for half in range(2):
                hs = slice(half * 512, (half + 1) * 512)
                P.stt(YO[q][:r, hs], PSB[6 + half][:r, :], s_[:r, 7:8], GPL[:r, hs], ALU.mult, ALU.mult,
                      reads=[psk(6 + half), sk_, "gpo"], writes=["yo%d" % q])
            for half in range(2):
                hs = slice(half * 512, (half + 1) * 512)
                P.act(SIG[:r, hs], PSB[4 + half][:r, :], AF.Sigmoid, reads=[psk(4 + half)], writes=["sig"])
            P.tt("dve", YO[q][:r, :], YO[q][:r, :], SIG[:r, :], ALU.mult, reads=["yo%d" % q, "sig"], writes=["yo%d" % q])
            P.tt("dve", YO[q][:r, :], YO[q][:r, :], X1[q][:r, :], ALU.add, reads=["yo%d" % q, "x1%d" % q], writes=["yo%d" % q])
            P.dma("sp", y_p[i * 128:(i + 1) * 128, :] if i < 16 else y_s[:, :], YO[q][:r, :], reads=["yo%d" % q], writes=["y"])

        load(0)
        load(1)
        for i in range(18):
            if i < 17:
                stage_A(i)
            if i >= 1:
                stage_C(i - 1)
            if i < 17:
                stage_B(i)
            if i >= 1:
                stage_D(i - 1)
            if i + 1 < 17 and i >= 1:
                load(i + 1)
    esw.close()
    P.barrier()
    return nc


_NC_CACHE = {}


def _consts():
    bf = ml_dtypes.bfloat16
    s = np.arange(128)[:, None]
    t = np.arange(128)[None, :]
    c = {}
    c["c_identb"] = np.eye(128, dtype=np.float32).astype(bf)
    c["c_identf"] = np.eye(128, dtype=np.float32)
    c["c_negmask"] = np.where(s <= t, 0.0, -30000.0).astype(np.float32).astype(bf)
    c["c_mask01"] = (s <= t).astype(np.float32)
    sel = np.zeros((8, 4, 128), np.float32)
    for h in range(4):
        sel[h, h, :] = 1.0
    c["c_sel8"] = sel.reshape(8, 512)
    sm = np.zeros((8, 16), np.float32)
    sm[4:, 0] = 1.0
    sm[:4, 1] = 1.0
    for k in range(8):
        sm[k, 2 + k % 4] = 1.0
    for k in range(4):
        sm[k, 6 + k] = 1.0
    c["c_small8"] = sm
    c["c_id16b"] = np.broadcast_to(np.eye(16, dtype=np.float32).reshape(1, 256), (128, 256)).copy()
    return c


def kernel(x_prompt, x_sample, state_C, state_n, state_m, state_conv, p_prompt, p_sample,
           g_pre, w_in, b_gates, conv_w, conv_b, g_head, g_lnv, b_lnv, w_s, b_s,
           w_pa, w_pb, w_out, g_post, w_ple, w_ple_gate, g_ple):
    f = lambda a: np.ascontiguousarray(np.asarray(a, dtype=np.float32))
    if "nc" not in _NC_CACHE:
        _NC_CACHE["nc"] = build_program()
    nc = _NC_CACHE["nc"]
    shared = dict(w_in=f(w_in[0]), g_pre=f(g_pre[0]), b_gates=f(b_gates[0]), conv_w=f(conv_w[0]), conv_b=f(conv_b[0]),
                  g_head=f(g_head[0]), g_lnv=f(g_lnv[0]), b_lnv=f(b_lnv[0]), w_s=f(w_s[0]), b_s=f(b_s[0]),
                  w_pa=f(w_pa[0]), w_pb=f(w_pb[0]), w_out=f(w_out[0]), g_post=f(g_post[0]), w_ple=f(w_ple[0]),
                  w_pg=f(w_ple_gate[0]), g_ple=f(g_ple[0]))
    shared.update(_consts())
    in_maps = []
    for c in range(8):
        sl = slice(16 * c, 16 * c + 16)
        d = dict(shared)
        d.update(xp=f(x_prompt[c]), xs=f(x_sample[sl, 0]), pp=f(p_prompt[0, c]), ps=f(p_sample[0, sl, 0]),
                 sC=f(state_C[0, sl]), sn=f(np.asarray(state_n[0, sl]).reshape(16, 1024)), sm=f(state_m[0, sl]),
                 sconv=f(state_conv[0, sl]))
        in_maps.append(d)
    res = run_bass_kernel_spmd(nc, in_maps, core_ids=list(range(8)))
    R = res.results
    cat = lambda k: np.concatenate([np.asarray(r[k]) for r in R], axis=0)
    stk = lambda k: np.stack([np.asarray(r[k]) for r in R], axis=0)
    y_prompt = stk("y_p")
    y_sample = cat("y_s").reshape(128, 1, 1024)
    C_prompt = stk("C_p")[None]
    n_prompt = stk("n_p")[None]
    m_prompt = stk("m_p").reshape(1, 8, 4)
    conv_prompt = stk("conv_p")[None]
    C_sample = cat("C_s")[None]
    n_sample = cat("n_s").reshape(1, 128, 4, 256)
    m_sample = cat("m_s")[None]
    conv_sample = cat("conv_s")[None]
    chunk_v = cat("vn_s").reshape(1, 128, 1, 1024)
    outs = (y_prompt, y_sample, C_prompt, n_prompt, m_prompt, conv_prompt, C_sample, n_sample, m_sample,
            conv_sample, chunk_v)
    return tuple(np.ascontiguousarray(o, dtype=np.float32) for o in outs)
```

```python
from contextlib import ExitStack
import numpy as np
import ml_dtypes
import concourse.bass as bass
import concourse.mybir as mybir
from concourse.bass_utils import run_bass_kernel_spmd

F32 = mybir.dt.float32
BF16 = mybir.dt.bfloat16
AF = mybir.ActivationFunctionType
ALU = mybir.AluOpType
AX = mybir.AxisListType

NT = 2064
EPS = 1e-6
C_QK, C_V, C_O, C_ZA, C_G, C_U, C_VB, C_ZB, C_GA, C_GB = 0, 2048, 3072, 4096, 5120, 5128, 6152, 7176, 8200, 9224


class Prog:
    def __init__(self, nc, n_dma_sems=40):
        self.nc = nc
        self.engs = {"pe": nc.tensor, "act": nc.scalar, "dve": nc.vector,
                     "pool": nc.gpsimd, "sp": nc.sync}
        self.psem = {e: nc.alloc_semaphore("prog_" + e) for e in self.engs}
        self.cnt = {e: 0 for e in self.engs}
        self.dsem = [nc.alloc_semaphore("dma_%d" % i) for i in range(n_dma_sems)]
        self.dcnt = [0] * n_dma_sems
        self.dnext = 0
        self.seen = {e: {} for e in self.engs}
        self.last_w = {}
        self.readers = {}
        self.semobj = {}
        for e in self.engs:
            self.semobj[("p", e)] = self.psem[e]
        for i, s in enumerate(self.dsem):
            self.semobj[("d", i)] = s

    def _deps(self, reads, writes):
        deps = []
        for r in reads:
            if r in self.last_w:
                deps.append(self.last_w[r])
        for w in writes:
            if w in self.last_w:
                deps.append(self.last_w[w])
            deps.extend(self.readers.get(w, []))
        return deps

    def _wait(self, eng, deps, same_engine=True):
        best = {}
        for (sid, val) in deps:
            if sid == ("p", eng) and not same_engine:
                continue
            if best.get(sid, 0) < val:
                best[sid] = val
        for sid, val in best.items():
            if self.seen[eng].get(sid, 0) >= val:
                continue
            self.engs[eng].wait_ge(self.semobj[sid], val)
            self.seen[eng][sid] = val

    def _record(self, tok, reads, writes):
        for w in writes:
            self.last_w[w] = tok
            self.readers[w] = []
        for r in reads:
            self.readers.setdefault(r, []).append(tok)

    @staticmethod
    def _psum_excl(reads, writes):
        extra = [k for k in reads if k.startswith("ps") and k[2:3].isdigit()]
        return list(reads), list(writes) + extra

    def op(self, eng, fn, reads=(), writes=(), same_engine=True):
        reads, writes = self._psum_excl(reads, writes)
        deps = self._deps(reads, writes)
        self._wait(eng, deps, same_engine)
        ins = fn(self.engs[eng])
        self.cnt[eng] += 1
        ins.then_inc(self.psem[eng], 1)
        tok = (("p", eng), self.cnt[eng])
        self._record(tok, reads, writes)
        return tok

    def dma(self, q, out, in_, reads=(), writes=(), **kw):
        deps = self._deps(reads, writes)
        i = self.dnext
        self.dnext = (self.dnext + 1) % len(self.dsem)
        if self.dcnt[i] > 0:
            deps.append((("d", i), self.dcnt[i]))
        self._wait(q, deps)
        ins = self.engs[q].dma_start(out=out, in_=in_, **kw)
        self.dcnt[i] += 16
        ins.then_inc(self.dsem[i], 16)
        tok = (("d", i), self.dcnt[i])
        self._record(tok, reads, writes)
        return tok

    def barrier(self):
        deps = []
        for e in self.engs:
            if self.cnt[e]:
                deps.append((("p", e), self.cnt[e]))
        for i in range(len(self.dsem)):
            if self.dcnt[i]:
                deps.append((("d", i), self.dcnt[i]))
        for e in self.engs:
            self._wait(e, deps)

    def mm(self, out, lhsT, rhs, start, stop, reads, writes):
        return self.op("pe", lambda e: e.matmul(out, lhsT, rhs, start=start, stop=stop),
                       reads=reads, writes=writes, same_engine=False)

    def tr(self, out, in_, ident, reads, writes):
        return self.op("pe", lambda e: e.transpose(out, in_, ident),
                       reads=reads, writes=writes, same_engine=False)

    def act(self, out, in_, func, reads, writes, **kw):
        return self.op("act", lambda e: e.activation(out=out, in_=in_, func=func, **kw),
                       reads=reads, writes=writes)

    def ts(self, eng, out, in0, s1, s2, op0, op1, reads, writes):
        if s2 is None:
            return self.op(eng, lambda e: e.tensor_scalar(out=out, in0=in0, scalar1=s1, scalar2=None, op0=op0),
                           reads=reads, writes=writes)
        return self.op(eng, lambda e: e.tensor_scalar(out=out, in0=in0, scalar1=s1, scalar2=s2, op0=op0, op1=op1),
                       reads=reads, writes=writes)

    def tt(self, eng, out, in0, in1, op, reads, writes):
        return self.op(eng, lambda e: e.tensor_tensor(out=out, in0=in0, in1=in1, op=op),
                       reads=reads, writes=writes)

    def stt(self, out, in0, scalar, in1, op0, op1, reads, writes):
        return self.op("dve", lambda e: e.scalar_tensor_tensor(out=out, in0=in0, scalar=scalar, in1=in1, op0=op0, op1=op1),
                       reads=reads, writes=writes)

    def cp(self, eng, out, in_, reads, writes):
        if eng == "act":
            return self.op("act", lambda e: e.copy(out=out, in_=in_), reads=reads, writes=writes)
        return self.op(eng, lambda e: e.tensor_copy(out=out, in_=in_), reads=reads, writes=writes)


def rows(t):
    return 128 if t < 16 else 16


def build_program(stop_after=99, dbg=None):
    nc = bass.Bass("TRN2", target_bir_lowering=False)

    def din(name, shape, dt=F32):
        return nc.dram_tensor(name, list(shape), dt, kind="ExternalInput").ap()

    def dout(name, shape):
        return nc.dram_tensor(name, list(shape), F32, kind="ExternalOutput").ap()

    xp = din("xp", [2048, 1024]); xs = din("xs", [16, 1024])
    pp = din("pp", [2048, 256]); psd = din("ps", [16, 256])
    sC = din("sC", [16, 4, 256, 256]); sn = din("sn", [16, 1024]); sm = din("sm", [16, 4])
    sconv = din("sconv", [16, 3, 2048])
    w_in = din("w_in", [1024, 10248]); g_pre = din("g_pre", [1024]); b_gates = din("b_gates", [8])
    conv_w = din("conv_w", [4, 2048]); conv_b = din("conv_b", [2048])
    g_head = din("g_head", [1024]); g_lnv = din("g_lnv", [1024]); b_lnv = din("b_lnv", [1024])
    w_s = din("w_s", [4, 128, 128]); b_s = din("b_s", [4, 128])
    w_pa = din("w_pa", [1024, 1024]); w_pb = din("w_pb", [1024, 1024]); w_out = din("w_out", [1024, 1024])
    g_post = din("g_post", [1024]); w_ple = din("w_ple", [256, 1024]); w_pg = din("w_pg", [1024, 1024])
    g_ple = din("g_ple", [1024])
    c_identb = din("c_identb", [128, 128], BF16); c_identf = din("c_identf", [128, 128])
    c_negmask = din("c_negmask", [128, 128], BF16); c_mask01 = din("c_mask01", [128, 128])
    c_sel8 = din("c_sel8", [8, 512]); c_small8 = din("c_small8", [8, 16])
    c_id16b = din("c_id16b", [128, 256])

    y_p = dout("y_p", [2048, 1024]); y_s = dout("y_s", [16, 1024])
    C_p = dout("C_p", [4, 256, 256]); n_p = dout("n_p", [4, 256]); m_p = dout("m_p", [4, 1])
    conv_p = dout("conv_p", [3, 2048])
    C_s = dout("C_s", [16, 4, 256, 256]); n_s = dout("n_s", [16, 1024]); m_s = dout("m_s", [16, 4])
    conv_s = dout("conv_s", [16, 3, 2048]); vn_s = dout("vn_s", [16, 1024])

    P = Prog(nc)
    sb = nc.alloc_sbuf_tensor
    _uid = [0]

    def sbt(name, shape, dt):
        _uid[0] += 1
        return nc.sbuf_tensor("%s_u%d" % (name, _uid[0]), shape, dt)

    HTf = sb("HT", [128, 8 * NT], BF16)
    QTf = sb("QT", [128, 8 * NT], BF16)
    KTf = sb("KT", [128, 8 * NT], BF16)
    HT = HTf[:, :].rearrange("p (k n) -> p k n", k=8)
    QT = QTf[:, :].rearrange("p (k n) -> p k n", k=8)
    KT = KTf[:, :].rearrange("p (k n) -> p k n", k=8)
    VV = sb("VV", [128, 17 * 1024], BF16)
    identb = sb("identb", [128, 128], BF16)
    identf = sb("identf", [128, 128], F32)
    negmask = sb("negmask", [128, 128], BF16)
    sel8 = sb("sel8", [8, 4, 128], F32)
    small8 = sb("small8", [8, 16], F32)
    id16b = sb("id16b", [128, 16, 16], F32)
    GPRE = sb("GPRE", [128, 8], F32)
    GHEAD = sb("GHEAD", [128, 8], F32)
    onesb = sb("onesb", [128, 2], BF16)
    zeros8 = sb("zeros8", [8, 128], F32)
    CW = sb("cw", [128, 16, 4], F32)
    CB = sb("cb", [128, 16], F32)
    esg = ExitStack()
    IG8 = esg.enter_context(sbt("IG8", [8, NT], F32))
    LF8 = esg.enter_context(sbt("LF8", [8, NT], F32))
    QKS = esg.enter_context(sbt("QKS", [16, 2048], F32))

    def Vt(t):
        return VV[:, t * 1024:(t + 1) * 1024]

    MT = VV[:, 0:8 * NT].rearrange("p (k n) -> p k n", k=8)

    PSB = [nc.alloc_psum_tensor("psb%d" % i, [128, 512], F32) for i in range(8)]

    def psk(i):
        return "ps%d" % i

    def psbf(i):
        return PSB[i][:, :].bitcast(BF16)

    P.dma("sp", identb[:, :], c_identb, writes=["const"])
    P.dma("sp", identf[:, :], c_identf, writes=["const"])
    P.dma("sp", negmask[:, :], c_negmask, writes=["const"])
    P.dma("sp", sel8[:, :, :], c_sel8.rearrange("k (h n) -> k h n", h=4), writes=["const"])
    P.dma("sp", small8[:, :], c_small8, writes=["const"])
    P.dma("sp", id16b[:, :, :], c_id16b.rearrange("p (a b) -> p a b", a=16), writes=["const"])
    P.op("dve", lambda e: e.memset(onesb[:, :], 1.0), writes=["const"])
    cw_pending = []
    def load_cw():
        with nc.allow_non_contiguous_dma(reason="tiny per-partition vectors"):
            for j in range(4):
                P.dma("sp", CW[:, :, j], conv_w[j].rearrange("(b p) -> p b", p=128), writes=["cw"])
            P.dma("sp", CB[:, :], conv_b.rearrange("(b p) -> p b", p=128), writes=["cw"])
            P.dma("sp", GPRE[:, :], g_pre.rearrange("(k p) -> p k", p=128), writes=["gvec"])
            P.dma("sp", GHEAD[:, :], g_head.rearrange("(k p) -> p k", p=128), writes=["gvec"])
            bgv = b_gates.rearrange("(a k o) -> a k o", a=2, o=1)
            P.dma("sp", small8[0:4, 10:11], bgv[0], writes=["bgv"])
            P.dma("sp", small8[4:8, 10:11], bgv[0], writes=["bgv"])
            P.dma("sp", small8[0:4, 11:12], bgv[1], writes=["bgv"])
            P.dma("sp", small8[4:8, 11:12], bgv[1], writes=["bgv"])
    P.op("dve", lambda e: e.memset(zeros8[:, :], 0.0), writes=["const"])
    HI, LO = small8[:, 0:1], small8[:, 1:2]
    MASKH = small8[:, 2:6]
    E8 = small8[:, 6:10]
    BGI, BGF = small8[:, 10:11], small8[:, 11:12]

    def rstd(es_tmp, ss, n, r, keys, extra=None):
        P.ts("dve", es_tmp, ss, 1.0 / n, EPS, ALU.mult, ALU.add, reads=keys, writes=keys)
        if extra is not None:
            P.tt("dve", es_tmp, es_tmp, extra, ALU.add, reads=keys, writes=keys)
        P.act(es_tmp, es_tmp, AF.Sqrt, reads=keys, writes=keys)
        P.op("dve", lambda e: e.reciprocal(out=r, in_=es_tmp), reads=keys, writes=keys)

    class WL:
        def __init__(self, es, nslab=2, ncol=256):
            self.stage = [es.enter_context(sbt("wstg%d" % i, [128, 8, ncol], F32)) for i in range(2)]
            self.slab = [es.enter_context(sbt("wslab%d" % i, [128, 8, ncol], BF16)) for i in range(nslab)]
            self.n = 0
            self.m = 0

        def load(self, w, c0, ncol, scale=None, kcn=8, dst=None, dstkey=None):
            i = self.n % 2
            self.n += 1
            sk = "wstg%d" % i
            P.dma("sp", self.stage[i][:, :kcn, :ncol],
                  w.rearrange("(k p) n -> p k n", p=128)[:, :, c0:c0 + ncol], writes=[sk])
            if dst is None:
                j = self.m % len(self.slab)
                self.m += 1
                dst = self.slab[j]
                dstkey = "wslab%d" % j
                dv = lambda kc: dst[:, kc, :ncol]
            else:
                dv = dst
            for kc in range(kcn):
                if scale is not None:
                    P.ts("pool", dv(kc), self.stage[i][:, kc, :ncol], scale[:, kc:kc + 1], 1.0, ALU.mult, ALU.mult,
                         reads=[sk, "const", "gvec"], writes=[dstkey])
                else:
                    P.cp("pool", dv(kc), self.stage[i][:, kc, :ncol], reads=[sk], writes=[dstkey])
            return dst, dstkey

    def proj_tok(ps_out, slab, skey, ncol, t, src, srckey, pskey, kcn=8, c0=0):
        r = rows(t)
        for kc in range(kcn):
            P.mm(ps_out, src[:, kc, t * 128:t * 128 + r], slab[:, kc, c0:c0 + ncol], kc == 0, kc == kcn - 1,
                 reads=[skey, srckey], writes=[pskey])

    def proj_feat(ps_out, slab, skey, c0, tok0, ntok, src, srckey, pskey, m=128):
        for kc in range(8):
            P.mm(ps_out, slab[:, kc, c0:c0 + m], src[:, kc, tok0:tok0 + ntok], kc == 0, kc == 7,
                 reads=[skey, srckey], writes=[pskey])

    def transpose_tile(srcf, srckey, t, dst, dstkey, bank, eng="act"):
        r = rows(t)
        pv = psbf(bank).rearrange("p (k n) -> p k n", k=8)
        for kc in range(8):
            P.tr(pv[:, kc, :r], srcf[:r, kc * 128:(kc + 1) * 128], identb[:r, :r],
                 reads=[srckey, "const"], writes=[psk(bank)])
        P.cp(eng, dst[:, :, t * 128:t * 128 + r], pv[:, :, :r], reads=[psk(bank)], writes=[dstkey])

    TOKG = [(0, 512), (512, 512), (1024, 512), (1536, 512), (2048, 16)]

    es1 = ExitStack()
    wl1 = WL(es1)
    with ExitStack() as es:
        load_cw_after = 8
        XT = [es.enter_context(sbt("xt%d" % i, [128, 1024], F32)) for i in range(2)]
        XB = [es.enter_context(sbt("xb%d" % i, [128, 1024], BF16)) for i in range(2)]
        JK = [es.enter_context(sbt("jk%d" % i, [128, 1024], BF16)) for i in range(2)]
        st = [es.enter_context(sbt("st0%d" % i, [128, 4], F32)) for i in range(2)]
        for t in range(17):
            r = rows(t)
            q = t % 2
            xt = XT[q]
            xk = "xt%d" % q
            sk0 = "st0%d" % q
            src = xp[t * 128:(t + 1) * 128, :] if t < 16 else xs[:, :]
            P.dma("sp", xt[:r, :], src, writes=[xk])
            P.act(JK[q][:r, :], xt[:r, :], AF.Square, reads=[xk], writes=["jk%d" % q, sk0], accum_out=st[q][:r, 0:1])
            rstd(st[q][:r, 1:2], st[q][:r, 0:1], 1024.0, st[q][:r, 2:3], [sk0])
            P.ts("dve", XB[q][:r, :], xt[:r, :], st[q][:r, 2:3], None, ALU.mult, None, reads=[xk, sk0], writes=["xb%d" % q])
            transpose_tile(XB[q], "xb%d" % q, t, HT, "HT", q, eng="dve")
            if t == load_cw_after:
                load_cw()
            if t == 12:
                pre_slab = wl1.load(w_in, C_QK, 256, scale=GPRE)
    P.barrier()
    if stop_after <= 1:
        return nc

    with es1 as es:
        wl = wl1
        QKP = [es.enter_context(sbt("qkpre%d" % i, [128, 2052], BF16)) for i in range(2)]
        DW = [es.enter_context(sbt("dw%d" % i, [128, 4, 128], BF16)) for i in range(2)]
        GT1 = es.enter_context(sbt("gt1", [8, 512], F32))
        T15 = es.enter_context(sbt("t15", [128, 256], F32))
        SQ = es.enter_context(sbt("sq", [16, 256], F32))
        SCV = es.enter_context(sbt("scv", [16, 3, 256], F32))
        CWB = es.enter_context(sbt("cwb", [16, 4, 256], F32))
        CBB = es.enter_context(sbt("cbb", [16, 256], F32))
        WG = es.enter_context(sbt("wg", [128, 8, 16], BF16))
        for i_ in range(2):
            P.op("dve", lambda e: e.memset(QKP[i_][:, 0:4], 0.0), writes=["qkpre%d_h" % i_])
        P.dma("sp", conv_s[:, 0:2, :], sconv[:, 1:3, :], writes=["conv_s01"])
        for sl in range(8):
            c0 = sl * 256
            slab, sk = pre_slab if sl == 0 else wl.load(w_in, C_QK + c0, 256, scale=GPRE)
            proj_tok(PSB[6][:, 0:256], slab, sk, 256, 15, HT, "HT", psk(6))
            P.cp("act", T15[:, :], PSB[6][:, 0:256], reads=[psk(6)], writes=["t15"])
            P.dma("sp", conv_p[:, c0:c0 + 256], T15[125:128, :], reads=["t15"], writes=["conv_p"])
            proj_tok(PSB[7][:16, 0:256], slab, sk, 256, 16, HT, "HT", psk(7))
            P.cp("act", SQ[:, :], PSB[7][:16, 0:256], reads=[psk(7)], writes=["sq"])
            P.dma("sp", conv_s[:, 2, c0:c0 + 256], SQ[:, :], reads=["sq"], writes=["conv_s2"])
            P.dma("sp", SCV[:, :, :], sconv[:, :, c0:c0 + 256], writes=["scv"])
            P.dma("sp", CWB[:, :, :], conv_w[:, c0:c0 + 256].partition_broadcast(16), writes=["cwb"])
            P.dma("sp", CBB[:, :], conv_b[c0:c0 + 256].partition_broadcast(16), writes=["cwb"])
            P.tt("dve", SQ[:, :], SQ[:, :], CWB[:, 3, :], ALU.mult, reads=["sq", "cwb"], writes=["sq"])
            P.tt("dve", SQ[:, :], SQ[:, :], CBB[:, :], ALU.add, reads=["sq", "cwb"], writes=["sq"])
            for j in range(3):
                P.tt("dve", SCV[:, j, :], SCV[:, j, :], CWB[:, j, :], ALU.mult, reads=["scv", "cwb"], writes=["scv"])
                P.tt("dve", SQ[:, :], SQ[:, :], SCV[:, j, :], ALU.add, reads=["sq", "scv"], writes=["sq"])
            P.act(SQ[:, :], SQ[:, :], AF.Silu, reads=["sq"], writes=["sq"])
            P.ts("dve", QKS[:, c0:c0 + 256], SQ[:, :], 1.0 if sl < 4 else 0.0625, None, ALU.mult, None,
                 reads=["sq"], writes=["QKS"])
            for bb in range(2):
                b = sl * 2 + bb
                q_ = b % 2
                qk = "qkpre%d" % q_
                for tt_ in range(4):
                    bank = tt_ % 2
                    proj_feat(PSB[bank][:, :], slab, sk, bb * 128, tt_ * 512, 512, HT, "HT", psk(bank))
                    P.cp("dve", QKP[q_][:, 4 + tt_ * 512:4 + (tt_ + 1) * 512], PSB[bank][:, :],
                         reads=[psk(bank)], writes=[qk + "_%d" % tt_])
                for j in range(4):
                    P.ts("pool", DW[q_][:, j, :], identb[:, :], CW[:, b, j:j + 1], 1.0, ALU.mult, ALU.mult,
                         reads=["const", "cw"], writes=["dw%d" % q_])
                dstT, dk_ = (QT, "QT") if b < 8 else (KT, "KT")
                for tt_ in range(4):
                    bank2 = 2 + tt_ % 2
                    rk = [qk + "_%d" % tt_, (qk + "_%d" % (tt_ - 1)) if tt_ > 0 else (qk + "_h"), "dw%d" % q_]
                    for j in range(4):
                        P.mm(PSB[bank2][:, :], DW[q_][:, j, :], QKP[q_][:, 1 + j + tt_ * 512:1 + j + (tt_ + 1) * 512], j == 0, j == 3,
                             reads=rk, writes=[psk(bank2)])
                    P.act(dstT[:, b % 8, tt_ * 512:(tt_ + 1) * 512], PSB[bank2][:, :], AF.Silu, reads=[psk(bank2), "cw"], writes=[dk_],
                          bias=CB[:, b:b + 1])
        i = wl.n % 2
        wl.n += 1
        sk = "wstg%d" % i
        P.dma("sp", wl.stage[i][:, :, 0:8], w_in.rearrange("(k p) n -> p k n", p=128)[:, :, C_G:C_G + 8], writes=[sk])
        for kc in range(8):
            for (d0, s0) in ((0, 0), (4, 0), (8, 4), (12, 4)):
                P.ts("pool", WG[:, kc, d0:d0 + 4], wl.stage[i][:, kc, s0:s0 + 4], GPRE[:, kc:kc + 1], 1.0,
                     ALU.mult, ALU.mult, reads=[sk, "const", "gvec"], writes=["wg"])
        for (t0, n) in TOKG:
            proj_feat(PSB[0][:8, :n], WG, "wg", 0, t0, n, HT, "HT", psk(0), m=8)
            proj_feat(PSB[1][:8, :n], WG, "wg", 8, t0, n, HT, "HT", psk(1), m=8)
            P.ts("dve", IG8[:, t0:t0 + n], PSB[0][:8, :n], BGI, None, ALU.add, None, reads=[psk(0), "const", "bgv"], writes=["IG8"])
            P.ts("dve", GT1[:, :n], PSB[1][:8, :n], BGF, None, ALU.add, None, reads=[psk(1), "const", "bgv"], writes=["gt1"])
            P.act(GT1[:, :n], GT1[:, :n], AF.Exp, reads=["gt1"], writes=["gt1"], scale=-1.0)
            P.act(GT1[:, :n], GT1[:, :n], AF.Ln, reads=["gt1"], writes=["gt1"], bias=1.0)
            P.ts("dve", LF8[:, t0:t0 + n], GT1[:, :n], -1.0, None, ALU.mult, None, reads=["gt1"], writes=["LF8"])
        for sl in range(4):
            slab, sk = wl.load(w_in, C_V + sl * 256, 256, scale=GPRE)
            for t in range(17):
                r = rows(t)
                bank = 2 + t % 2
                proj_tok(PSB[bank][:r, 0:256], slab, sk, 256, t, HT, "HT", psk(bank))
                P.cp("act" if t % 2 == 0 else "dve", Vt(t)[:r, sl * 256:(sl + 1) * 256], PSB[bank][:r, 0:256], reads=[psk(bank)], writes=["V%d" % t])
    P.barrier()
    if stop_after <= 2:
        return nc

    scr1 = nc.dram_tensor("scr_r1", [4, 2048], F32).ap()
    scr2 = nc.dram_tensor("scr_r2", [4, 2048], F32).ap()
    with ExitStack() as es:
        def T(name, shape, dt=F32):
            return es.enter_context(sbt(name, shape, dt))
        AT_all = T("AT_all", [128, 64]); EMT_all = T("EMT_all", [128, 64])
        NEGF = T("negf", [128, 128])
        P.dma("sp", NEGF[:, :], c_mask01, writes=["negf"])
        P.ts("dve", NEGF[:, :], NEGF[:, :], 30000.0, -30000.0, ALU.mult, ALU.add, reads=["negf"], writes=["negf"])
        with ExitStack() as esu:
            M8 = esu.enter_context(sbt("M8", [8, 2048], F32))
            FA = esu.enter_context(sbt("FA", [8, 2048], F32))
            P.op("dve", lambda e: e.tensor_tensor_scan(out=M8[:, :], data0=LF8[:, 0:2048], data1=IG8[:, 0:2048],
                                                       initial=0.0, op0=ALU.add, op1=ALU.max),
                 reads=["LF8", "IG8"], writes=["M8"])
            for c in range(16):
                cs = slice(c * 128, (c + 1) * 128)
                P.op("dve", lambda e: e.tensor_tensor_scan(out=FA[:, cs], data0=LF8[:, cs], data1=zeros8[:, :], initial=0.0,
                                                           op0=ALU.add, op1=ALU.add), reads=["LF8", "const"], writes=["FA"])
            P.tt("dve", IG8[:, 0:2048], IG8[:, 0:2048], FA[:, :], ALU.subtract, reads=["IG8", "FA"], writes=["IG8"])
            P.tt("dve", FA[:, :], FA[:, :], M8[:, :], ALU.subtract, reads=["FA", "M8"], writes=["FA"])
            for c in range(16):
                cs = slice(c * 128, (c + 1) * 128)
                P.mm(PSB[7][:, c * 4:c * 4 + 4], M8[:, cs], E8, True, True, reads=["M8", "const"], writes=[psk(7)])
                P.mm(PSB[6][:, c * 4:c * 4 + 4], IG8[:, cs], E8, True, True, reads=["IG8", "const"], writes=[psk(6)])
            P.act(EMT_all[:, :], PSB[7][:, 0:64], AF.Exp, reads=[psk(7)], writes=["emt"], scale=-1.0)
            P.cp("dve", AT_all[:, :], PSB[6][:, 0:64], reads=[psk(6)], writes=["at"])
            P.dma("sp", m_p, M8[0:4, 2047:2048], reads=["M8"], writes=["m_p"])
            for c in range(15, 0, -1):
                cs = slice(c * 128, (c + 1) * 128)
                P.ts("dve", M8[:, cs], FA[:, cs], M8[:, c * 128 - 1:c * 128], None, ALU.add, None, reads=["FA", "M8"], writes=["M8"])
            P.cp("dve", M8[:, 0:128], FA[:, 0:128], reads=["FA", "M8"], writes=["M8"])
            P.dma("sp", scr1, FA[0:4, :], reads=["FA"], writes=["scr1"])
            P.dma("sp", scr2, M8[0:4, :], reads=["M8"], writes=["scr2"])
            P.barrier()
        DT = [T("dT%d" % i, [128, 4, 128]) for i in range(2)]
        WBC = [T("wbc%d" % i, [128, 4, 128]) for i in range(2)]
        S = [T("S%d" % i, [128, 4, 128], BF16) for i in range(2)]
        QS = [T("QS%d" % i, [128, 8, 128], BF16) for i in range(2)]
        KW = [T("KW%d" % i, [128, 4, 256], BF16) for i in range(2)]
        CF = T("CF", [128, 2, 4, 256]); CBf = T("CBf", [128, 2, 4, 256], BF16)
        NF = T("NF", [128, 4, 2]); NB = T("NB", [128, 4, 2], BF16)
        SMT = T("SMT", [128, 32])
        JK2 = T("jk2", [128, 256], BF16)
        DCB = {(0, 0): 0, (0, 1): 1, (1, 0): 3, (1, 1): 0}

        def x_dma(c):
            q = c % 2
            cs = slice(c * 128, (c + 1) * 128)
            P.dma("sp", DT[q][:, :, :], scr1[:, cs].partition_broadcast(128), reads=["scr1"], writes=["dT%d" % q])
            P.dma("sp", WBC[q][:, :, :], scr2[:, cs].partition_broadcast(128), reads=["scr2"], writes=["wbc%d" % q])

        def x_pre(c):
            q = c % 2
            dk = "dT%d" % q
            P.tt("pool", DT[q][:, :, :], DT[q][:, :, :], NEGF[:, :].unsqueeze(1).to_broadcast([128, 4, 128]), ALU.add,
                 reads=[dk, "negf"], writes=[dk])
            for h in range(4):
                P.act(DT[q][:, h, :], DT[q][:, h, :], AF.Exp, reads=[dk, "at"], writes=[dk], bias=AT_all[:, c * 4 + h:c * 4 + h + 1])
            P.act(WBC[q][:, :, :], WBC[q][:, :, :], AF.Exp, reads=["wbc%d" % q], writes=["wbc%d" % q])

        def x_pe(c):
            cs = slice(c * 128, (c + 1) * 128)
            ST = PSB[2][:, :].rearrange("p (h n) -> p h n", h=4)
            for h in range(4):
                for j in range(2):
                    P.mm(ST[:, h, :], KT[:, 2 * h + j, cs], QT[:, 2 * h + j, cs], j == 0, j == 1,
                         reads=["KT", "QT"], writes=[psk(2)])
            KTR = psbf(6)
            for hj in range(8):
                P.tr(KTR[:, hj * 128:(hj + 1) * 128], KT[:, hj, cs], identb[:, :], reads=["KT", "const"], writes=[psk(6)])

        def x_post(c):
            q = c % 2
            cs = slice(c * 128, (c + 1) * 128)
            ST = PSB[2][:, :].rearrange("p (h n) -> p h n", h=4)
            KTR = psbf(6)
            P.stt(S[q][:, :, :], ST, 0.0625, DT[q][:, :, :], ALU.mult, ALU.mult, reads=[psk(2), "dT%d" % q], writes=["S%d" % q])
            for h in range(4):
                P.ts("dve", KW[q][:, h, :], KTR[:, h * 256:(h + 1) * 256], DT[q][:, h, 127:128], 0.0625, ALU.mult, ALU.mult,
                     reads=[psk(6), "dT%d" % q], writes=["KW%d" % q])
            if c > 0:
                for h in range(4):
                    P.tt("pool", QS[q][:, 2 * h:2 * h + 2, :], QT[:, 2 * h:2 * h + 2, cs],
                         WBC[q][:, h:h + 1, :].to_broadcast([128, 2, 128]), ALU.mult, reads=["QT", "wbc%d" % q], writes=["QS%d" % q])

        def y_dc(c):
            q = c % 2
            vk = "V%d" % c
            for j in range(2):
                for h in range(4):
                    bk, col = DCB[(j, h // 2)], (h % 2) * 256
                    P.mm(PSB[bk][:, col:col + 256], KW[q][:, h, j * 128:(j + 1) * 128], Vt(c)[:, h * 256:(h + 1) * 256],
                         True, True, reads=["KW%d" % q, vk], writes=[psk(bk)])
                    P.mm(PSB[7][:, 16 + h * 2 + j:17 + h * 2 + j], KW[q][:, h, j * 128:(j + 1) * 128], onesb[:, 0:1],
                         True, True, reads=["KW%d" % q, "const"], writes=[psk(7)])
                    if c == 0:
                        P.cp("dve", CF[:, j, h, :], PSB[bk][:, col:col + 256], reads=[psk(bk)], writes=["CF"])
                    else:
                        P.stt(CF[:, j, h, :], CF[:, j, h, :], WBC[q][:, h, 127:128], PSB[bk][:, col:col + 256],
                              ALU.mult, ALU.add, reads=["CF", "wbc%d" % q, psk(bk)], writes=["CF"])
            DN = PSB[7][:, 16:24].rearrange("p (h j) -> p h j", h=4)
            for h in range(4):
                if c == 0:
                    P.cp("dve", NF[:, h, :], DN[:, h, :], reads=[psk(7)], writes=["NF"])
                else:
                    P.stt(NF[:, h, :], NF[:, h, :], WBC[q][:, h, 127:128], DN[:, h, :], ALU.mult, ALU.add,
                          reads=["NF", "wbc%d" % q, psk(7)], writes=["NF"])

        def y_num(c):
            q = c % 2
            vk = "V%d" % c
            for h in range(4):
                nb_, col = 4 + h // 2, (h % 2) * 256
                P.mm(PSB[nb_][:, col:col + 256], S[q][:, h, :], Vt(c)[:, h * 256:(h + 1) * 256], True, c == 0,
                     reads=["S%d" % q, vk], writes=[psk(nb_)])
                if c > 0:
                    for j in range(2):
                        P.mm(PSB[nb_][:, col:col + 256], QS[q][:, 2 * h + j, :], CBf[:, j, h, :], False, j == 1,
                             reads=["QS%d" % q, "CBf"], writes=[psk(nb_)])
            for h in range(4):
                P.mm(PSB[7][:, 8 + h:9 + h], S[q][:, h, :], onesb[:, 0:1], True, c == 0, reads=["S%d" % q, "const"], writes=[psk(7)])
                if c > 0:
                    for j in range(2):
                        P.mm(PSB[7][:, 8 + h:9 + h], QS[q][:, 2 * h + j, :], NB[:, h, j:j + 1], False, j == 1,
                             reads=["QS%d" % q, "NB"], writes=[psk(7)])

        def y_h(c):
            vk = "V%d" % c
            EM = EMT_all[:, c * 4:c * 4 + 4]
            for h in range(4):
                nb_, col = 4 + h // 2, (h % 2) * 256
                P.act(JK2[:, :], PSB[nb_][:, col:col + 256], AF.Square, reads=[psk(nb_)], writes=["jk2", "smt_s"],
                      accum_out=SMT[:, 8 + h:9 + h])
            P.ts("dve", SMT[:, 12:16], PSB[7][:, 8:12], -1.0, None, ALU.mult, None, reads=[psk(7)], writes=["smt_t"])
            P.tt("dve", SMT[:, 4:8], PSB[7][:, 8:12], SMT[:, 12:16], ALU.max, reads=[psk(7), "smt_t"], writes=["smt_d"])
            P.tt("dve", SMT[:, 4:8], SMT[:, 4:8], EM, ALU.max, reads=["smt_d", "emt"], writes=["smt_d"])
            P.tt("dve", SMT[:, 4:8], SMT[:, 4:8], SMT[:, 4:8], ALU.mult, reads=["smt_d"], writes=["smt_d"])
            P.ts("dve", SMT[:, 12:16], SMT[:, 8:12], 1.0 / 256.0, None, ALU.mult, None, reads=["smt_s"], writes=["smt_t"])
            P.stt(SMT[:, 12:16], SMT[:, 4:8], EPS, SMT[:, 12:16], ALU.mult, ALU.add, reads=["smt_t", "smt_d"], writes=["smt_t"])
            P.act(SMT[:, 12:16], SMT[:, 12:16], AF.Sqrt, reads=["smt_t"], writes=["smt_t"])
            P.op("dve", lambda e: e.reciprocal(out=SMT[:, 16:20], in_=SMT[:, 12:16]), reads=["smt_t"], writes=["smt_c"])
            for h in range(4):
                nb_, col = 4 + h // 2, (h % 2) * 256
                P.ts("dve", Vt(c)[:, h * 256:(h + 1) * 256], PSB[nb_][:, col:col + 256], SMT[:, 16 + h:17 + h], None,
                     ALU.mult, None, reads=[psk(nb_), "smt_c"], writes=[vk])

        def y_cast(c):
            if c < 15:
                P.cp("act", CBf[:, :, :, :], CF[:, :, :, :], reads=["CF"], writes=["CBf"])
                P.cp("act", NB[:, :, :], NF[:, :, :], reads=["NF"], writes=["NB"])

        x_dma(0)
        x_dma(1)
        x_pre(0)
        x_pe(0)
        x_post(0)
        for c in range(16):
            y_dc(c)
            if c + 1 < 16:
                x_pre(c + 1)
                x_pe(c + 1)
            if c + 1 < 16:
                x_post(c + 1)
            if c + 2 < 16:
                x_dma(c + 2)
            y_num(c)
            y_h(c)
            y_cast(c)
        for j in range(2):
            P.dma("sp", C_p[:, j * 128:(j + 1) * 128, :].rearrange("h p v -> p h v"), CF[:, j, :, :], reads=["CF"], writes=["C_p"])
        with nc.allow_non_contiguous_dma(reason="tiny state vector"):
            P.dma("sp", n_p.rearrange("h (j p) -> p h j", p=128), NF[:, :, :], reads=["NF"], writes=["n_p"])
    P.barrier()
    if stop_after <= 3:
        return nc

    with ExitStack() as es:
        def T(name, shape, dt=F32):
            return es.enter_context(sbt(name, shape, dt))
        CS = [T("cs%d" % i, [128, 4, 2, 256]) for i in range(2)]
        KDd = T("kdd", [16, 1024])
        G = T("gs8", [8, 6, 16])
        SM = T("smts", [16, 40])
        WPB = T("wpb", [128, 4, 16])
        SN = T("snn", [16, 1024]); TMPF = T("tmpf", [16, 1024]); KD = T("kd", [16, 1024])
        KDs = [T("kds%d" % i, [16, 1024], BF16) for i in range(2)]
        QTs = T("qts", [128, 8, 16]); QMs = [T("qms%d" % i, [128, 8, 16], BF16) for i in range(2)]
        CSB = [T("csb%d" % i, [128, 4, 2, 256], BF16) for i in range(2)]
        JK3 = T("jk3", [16, 256], BF16)
        sc_ = slice(2048, 2064)
        with nc.allow_non_contiguous_dma(reason="tiny state vector"):
            P.dma("sp", G[0:4, 0, :], sm.rearrange("s h -> h s"), writes=["gs"])
            P.dma("sp", G[4:8, 0, :], sm.rearrange("s h -> h s"), writes=["gs"])
        P.dma("sp", SN[:, :], sn, writes=["snn"])
        P.tt("dve", G[:, 1, :], LF8[:, sc_], G[:, 0, :], ALU.add, reads=["LF8", "gs"], writes=["gs"])
        P.tt("dve", G[:, 2, :], G[:, 1, :], IG8[:, sc_], ALU.max, reads=["IG8", "gs"], writes=["gs"])
        P.tt("dve", G[:, 3, :], IG8[:, sc_], G[:, 2, :], ALU.subtract, reads=["IG8", "gs"], writes=["gs"])
        P.act(G[:, 3, :], G[:, 3, :], AF.Exp, reads=["gs"], writes=["gs"])
        P.tt("dve", G[:, 4, :], G[:, 1, :], G[:, 2, :], ALU.subtract, reads=["gs"], writes=["gs"])
        P.act(G[:, 4, :], G[:, 4, :], AF.Exp, reads=["gs"], writes=["gs"])
        for i_, gi in enumerate((2, 3, 4)):
            P.mm(PSB[7][:16, 4 * i_:4 * i_ + 4], G[:, gi, :], E8, True, True, reads=["gs", "const"], writes=[psk(7)])
        P.cp("dve", SM[:, 0:12], PSB[7][:16, 0:12], reads=[psk(7)], writes=["sm"])
        P.dma("sp", m_s, SM[:, 0:4], reads=["sm"], writes=["m_s"])
        for h in range(4):
            P.mm(PSB[7][:, 32 + h * 16:48 + h * 16], sel8[:, h, :], G[:, 4, :], True, True, reads=["gs", "const"], writes=[psk(7)])
        P.cp("dve", WPB[:, :, :], PSB[7][:, 32:96].rearrange("p (h s) -> p h s", h=4), reads=[psk(7)], writes=["wpb"])
        Qs, Ks = QKS[:, 0:1024], QKS[:, 1024:2048]
        V16 = Vt(16)
        P.tt("dve", TMPF[:, :], Qs, Ks, ALU.mult, reads=["QKS"], writes=["tmpf"])
        P.op("dve", lambda e: e.tensor_reduce(out=SM[:, 12:16], in_=TMPF[:, :].rearrange("p (h d) -> p h d", h=4),
                                              axis=AX.X, op=ALU.add), reads=["tmpf"], writes=["sm"])
        P.tt("dve", TMPF[:, :], Qs, SN[:, :], ALU.mult, reads=["QKS", "snn"], writes=["tmpf"])
        P.op("dve", lambda e: e.tensor_reduce(out=SM[:, 16:20], in_=TMPF[:, :].rearrange("p (h d) -> p h d", h=4),
                                              axis=AX.X, op=ALU.add), reads=["tmpf"], writes=["sm"])
        P.tt("dve", SM[:, 12:16], SM[:, 12:16], SM[:, 4:8], ALU.mult, reads=["sm"], writes=["sm"])
        P.tt("dve", SM[:, 20:24], SM[:, 16:20], SM[:, 8:12], ALU.mult, reads=["sm"], writes=["sm"])
        P.tt("dve", SM[:, 20:24], SM[:, 20:24], SM[:, 12:16], ALU.add, reads=["sm"], writes=["sm"])
        P.ts("dve", SM[:, 28:32], SM[:, 20:24], -1.0, None, ALU.mult, None, reads=["sm"], writes=["sm"])
        P.tt("dve", SM[:, 20:24], SM[:, 20:24], SM[:, 28:32], ALU.max, reads=["sm"], writes=["sm"])
        P.act(SM[:, 28:32], SM[:, 0:4], AF.Exp, reads=["sm"], writes=["sm"], scale=-1.0)
        P.tt("dve", SM[:, 20:24], SM[:, 20:24], SM[:, 28:32], ALU.max, reads=["sm"], writes=["sm"])
        for h in range(4):
            hs = slice(h * 256, (h + 1) * 256)
            P.ts("dve", KD[:, hs], Ks[:, hs], SM[:, 4 + h:5 + h], None, ALU.mult, None, reads=["QKS", "sm"], writes=["kd"])
            P.stt(SN[:, hs], SN[:, hs], SM[:, 8 + h:9 + h], KD[:, hs], ALU.mult, ALU.add, reads=["snn", "sm", "kd"], writes=["snn"])
        P.dma("sp", n_s, SN[:, :], reads=["snn"], writes=["n_s"])
        P.cp("dve", KDd[:, :].rearrange("p (h j q) -> p h j q", h=4, j=2), KD[:, :].rearrange("p (h q j) -> p h j q", h=4, j=2),
             reads=["kd"], writes=["kdd"])
        P.cp("dve", TMPF[:, :].rearrange("p (h j q) -> p h j q", h=4, j=2), Qs.rearrange("p (h q j) -> p h j q", h=4, j=2),
             reads=["QKS", "tmpf"], writes=["tmpf"])
        QTP = PSB[6][:, 0:128].rearrange("p (k s) -> p k s", k=8)
        for hj in range(8):
            P.tr(QTP[:, hj, :], TMPF[:, hj * 128:(hj + 1) * 128], identf[:16, :16], reads=["tmpf", "const"], writes=[psk(6)])
        P.cp("dve", QTs[:, :, :], QTP, reads=[psk(6)], writes=["qts"])
        def cs_load(s_):
            b = s_ % 2
            P.dma("sp", CS[b][:, :, :, :].rearrange("p h j v -> p h (j v)"), sC[s_].rearrange("h (p j) v -> p h (j v)", j=2),
                  writes=["cs%d" % b])

        def cast(s_):
            b = s_ % 2
            P.cp("act", CSB[b][:, :, :, :], CS[b][:, :, :, :], reads=["cs%d" % b], writes=["csb%d" % b])

        def prep(s_):
            q = s_ % 2
            P.tt("dve", QMs[q][:, :, :], QTs[:, :, :], id16b[:, s_:s_ + 1, :].to_broadcast([128, 8, 16]), ALU.mult,
                 reads=["qts", "const"], writes=["qms%d" % q])
            P.ts("dve", KDs[q][:, :], KDd[:, :], identf[:16, s_:s_ + 1], None,
                 ALU.mult, None, reads=["kdd", "const"], writes=["kds%d" % q])
        cs_load(0)
        cs_load(1)
        prep(0)
        cast(0)
        for s_ in range(16):
            b = s_ % 2
            q = s_ % 2
            for h in range(4):
                for j in range(2):
                    P.mm(PSB[h][:16, 0:256], QMs[q][:, 2 * h + j, :], CSB[b][:, h, j, :], s_ == 0 and j == 0, s_ == 15 and j == 1,
                         reads=["qms%d" % q, "csb%d" % b], writes=[psk(h)])
            for j in range(2):
                for h in range(4):
                    bk, col = 4 + 2 * j + h // 2, (h % 2) * 256
                    P.mm(PSB[bk][:, col:col + 256], KDs[q][:, h * 256 + j * 128:h * 256 + (j + 1) * 128],
                         V16[:16, h * 256:(h + 1) * 256], True, True, reads=["kds%d" % q, "V16"], writes=[psk(bk)])
            if s_ + 1 < 16:
                cast(s_ + 1)
                prep(s_ + 1)
            for j in range(2):
                for h in range(4):
                    bk, col = 4 + 2 * j + h // 2, (h % 2) * 256
                    P.stt(CS[b][:, h, j, :], CS[b][:, h, j, :], WPB[:, h, s_:s_ + 1], PSB[bk][:, col:col + 256],
                          ALU.mult, ALU.add, reads=["cs%d" % b, "wpb", psk(bk)], writes=["cs%d" % b])
            P.dma("act", C_s[s_].rearrange("h (p j) v -> p h (j v)", j=2), CS[b][:, :, :, :].rearrange("p h j v -> p h (j v)"),
                  reads=["cs%d" % b], writes=["C_s"])
            if s_ + 2 < 16:
                cs_load(s_ + 2)
        for h in range(4):
            hs = slice(h * 256, (h + 1) * 256)
            P.ts("dve", TMPF[:, hs], V16[:16, hs], SM[:, 12 + h:13 + h], None, ALU.mult, None, reads=["V16", "sm"], writes=["tmpf"])
            P.stt(TMPF[:, hs], PSB[h][:16, 0:256], SM[:, 8 + h:9 + h], TMPF[:, hs], ALU.mult, ALU.add,
                  reads=[psk(h), "sm", "tmpf"], writes=["tmpf"])
            P.act(JK3[:, :], TMPF[:, hs], AF.Square, reads=["tmpf"], writes=["jk3", "sm2"], accum_out=SM[:, 24 + h:25 + h])
        P.tt("dve", SM[:, 28:32], SM[:, 20:24], SM[:, 20:24], ALU.mult, reads=["sm", "sm2"], writes=["sm"])
        P.ts("dve", SM[:, 28:32], SM[:, 28:32], EPS, None, ALU.mult, None, reads=["sm"], writes=["sm"])
        P.ts("dve", SM[:, 24:28], SM[:, 24:28], 1.0 / 256.0, None, ALU.mult, None, reads=["sm", "sm2"], writes=["sm"])
        P.tt("dve", SM[:, 28:32], SM[:, 28:32], SM[:, 24:28], ALU.add, reads=["sm"], writes=["sm"])
        P.act(SM[:, 28:32], SM[:, 28:32], AF.Sqrt, reads=["sm"], writes=["sm"])
        P.op("dve", lambda e: e.reciprocal(out=SM[:, 32:36], in_=SM[:, 28:32]), reads=["sm"], writes=["sm"])
        for h in range(4):
            hs = slice(h * 256, (h + 1) * 256)
            P.ts("dve", V16[:16, hs], TMPF[:, hs], SM[:, 32 + h:33 + h], None, ALU.mult, None, reads=["tmpf", "sm"], writes=["V16"])
    P.barrier()
    esg.close()
    if stop_after <= 4:
        return nc

    with ExitStack() as es:
        wl = WL(es)
        TMP = es.enter_context(sbt("tmpE", [128, 256], F32))
        for (cbase, fn) in ((C_O, AF.Sigmoid), (C_ZA, AF.Silu)):
            for sl in range(4):
                slab, sk = wl.load(w_in, cbase + sl * 256, 256, scale=GPRE)
                for t in range(17):
                    r = rows(t)
                    bank = t % 2
                    vk = "V%d" % t
                    proj_tok(PSB[bank][:r, 0:256], slab, sk, 256, t, HT, "HT", psk(bank))
                    P.act(TMP[:r, :], PSB[bank][:r, 0:256], fn, reads=[psk(bank)], writes=["tmpE"])
                    P.tt("dve", Vt(t)[:r, sl * 256:(sl + 1) * 256], Vt(t)[:r, sl * 256:(sl + 1) * 256], TMP[:r, :], ALU.mult,
                         reads=[vk, "tmpE"], writes=[vk])
        for t in range(17):
            transpose_tile(Vt(t), "V%d" % t, t, QT, "QT", 2 + t % 2)
    P.barrier()
    if stop_after <= 5:
        return nc

    with ExitStack() as es:
        wl = WL(es)
        def T(name, shape, dt=F32):
            return es.enter_context(sbt(name, shape, dt))
        WST = T("wst", [128, 4, 128], BF16); WSL = T("wsl", [128, 128]); M01 = T("m01", [128, 128])
        BST = T("bst", [128, 4]); W00 = T("w00", [16, 8]); RS = T("rs", [128, 4])
        GL = T("gl", [128, 1024]); BL = T("bl", [128, 1024]); RB = T("rb", [128, 1024])
        LNS = T("lns", [128, 17, 4, 2]); MV = [T("mv%d" % i, [128, 8]) for i in range(2)]
        TMPB = T("tmpb", [128, 256], BF16)
        TMP = [T("tmpG%d" % i, [128, 512]) for i in range(2)]
        P.dma("sp", GL[:, :], g_lnv.partition_broadcast(128), writes=["gl"])
        P.dma("sp", BL[:, :], b_lnv.partition_broadcast(128), writes=["gl"])
        P.dma("sp", M01[:, :], c_mask01, writes=["m01"])
        with nc.allow_non_contiguous_dma(reason="tiny vectors"):
            P.dma("sp", BST[:, :], b_s.rearrange("g t -> t g"), writes=["bst"])
            P.dma("sp", W00[:, 0:4], w_s[:, 0, 0].partition_broadcast(16), writes=["w00"])
            P.dma("sp", W00[:, 4:8], b_s[:, 0].partition_broadcast(16), writes=["w00"])
        for g in range(4):
            P.dma("sp", WSL[:, :], w_s[g], writes=["wsl"])
            P.tr(PSB[7][:, 0:128], WSL[:, :], identf[:, :], reads=["wsl", "const"], writes=[psk(7)])
            P.tt("dve", WST[:, g, :], PSB[7][:, 0:128], M01[:, :], ALU.mult, reads=[psk(7), "m01"], writes=["wst"])
        for g in range(4):
            P.mm(PSB[7][:, 256 + g:257 + g], WST[:, g, :], onesb[:, 0:1], True, True, reads=["wst", "const"], writes=[psk(7)])
        P.cp("dve", RS[:, :], PSB[7][:, 256:260], reads=[psk(7)], writes=["rs"])
        for g in range(4):
            gs = slice(g * 256, (g + 1) * 256)
            P.ts("dve", RB[:, gs], BL[:, gs], RS[:, g:g + 1], BST[:, g:g + 1], ALU.mult, ALU.add, reads=["gl", "rs", "bst"], writes=["rb"])
        for sl in range(4):
            slab, sk = wl.load(w_in, C_VB + sl * 256, 256, scale=GPRE)
            for t in range(17):
                r = rows(t)
                bank = t % 2
                vk = "V%d" % t
                proj_tok(PSB[bank][:r, 0:256], slab, sk, 256, t, HT, "HT", psk(bank))
                P.act(Vt(t)[:r, sl * 256:(sl + 1) * 256], PSB[bank][:r, 0:256], AF.Copy, reads=[psk(bank)],
                      writes=[vk, "lns%d" % t], accum_out=LNS[:r, t, sl, 0:1])
                P.act(TMPB[:r, :], PSB[bank][:r, 0:256], AF.Square, reads=[psk(bank)],
                      writes=["tmpb", "lns%d" % t], accum_out=LNS[:r, t, sl, 1:2])

        def pre(t):
            r = rows(t); q = t % 2
            mv = MV[q]; mk = "mv%d" % q
            vk = "V%d" % t
            P.op("dve", lambda e: e.tensor_reduce(out=mv[:r, 0:2], in_=LNS[:r, t, :, :].rearrange("p a b -> p b a"),
                                                  axis=AX.X, op=ALU.add), reads=["lns%d" % t], writes=[mk])
            P.ts("dve", mv[:r, 0:2], mv[:r, 0:2], 1.0 / 1024.0, None, ALU.mult, None, reads=[mk], writes=[mk])
            P.tt("dve", mv[:r, 2:3], mv[:r, 0:1], mv[:r, 0:1], ALU.mult, reads=[mk], writes=[mk])
            P.tt("dve", mv[:r, 1:2], mv[:r, 1:2], mv[:r, 2:3], ALU.subtract, reads=[mk], writes=[mk])
            P.ts("dve", mv[:r, 2:3], mv[:r, 1:2], EPS, None, ALU.add, None, reads=[mk], writes=[mk])
            P.act(mv[:r, 2:3], mv[:r, 2:3], AF.Sqrt, reads=[mk], writes=[mk])
            P.op("dve", lambda e: e.reciprocal(out=mv[:r, 3:4], in_=mv[:r, 2:3]), reads=[mk], writes=[mk])
            P.stt(mv[:r, 4:5], mv[:r, 0:1], -1.0, mv[:r, 3:4], ALU.mult, ALU.mult, reads=[mk], writes=[mk])
            P.act(Vt(t)[:r, :], Vt(t)[:r, :], AF.Identity, reads=[vk, mk], writes=[vk], scale=mv[:r, 3:4], bias=mv[:r, 4:5])

        def mix(t):
            vk = "V%d" % t
            if t == 16:
                VNS = TMP[0][:16, :].rearrange("p (a b) -> p a b", a=2)
                return
            for g in range(4):
                bk, col = 2 + g // 2, (g % 2) * 256
                P.mm(PSB[bk][:, col:col + 256], WST[:, g, :], Vt(t)[:, g * 256:(g + 1) * 256], True, True,
                     reads=["wst", vk], writes=[psk(bk)])

        def mix_evac(t):
            vk = "V%d" % t
            for hf in range(2):
                hs = slice(hf * 512, (hf + 1) * 512)
                tm = TMP[hf]; tk_ = "tmpG%d" % hf
                P.tt("dve", tm[:, :], PSB[2 + hf][:, :], GL[:, hs], ALU.mult, reads=[psk(2 + hf), "gl"], writes=[tk_])
                P.tt("dve", Vt(t)[:, hs], tm[:, :], RB[:, hs], ALU.add, reads=[tk_, "rb"], writes=[vk])

        def sample_mix():
            vk = "V16"
            VN = [TMP[0][:16, :], TMP[1][:16, :]]
            for hf in range(2):
                hs = slice(hf * 512, (hf + 1) * 512)
                tk_ = "tmpG%d" % hf
                P.tt("dve", VN[hf], Vt(16)[:16, hs], GL[:16, hs], ALU.mult, reads=[vk, "gl"], writes=[tk_])
                P.tt("dve", VN[hf], VN[hf], BL[:16, hs], ALU.add, reads=[tk_, "gl"], writes=[tk_])
                P.dma("sp", vn_s[:, hs], VN[hf], reads=[tk_], writes=["vn_s"])
                V3 = Vt(16)[:16, hs].rearrange("p (g d) -> p g d", g=2)
                VN3 = VN[hf].rearrange("p (g d) -> p g d", g=2)
                P.tt("dve", VN3, VN3, W00[:, 2 * hf:2 * hf + 2].unsqueeze(2).to_broadcast([16, 2, 256]), ALU.mult,
                     reads=[tk_, "w00"], writes=[tk_])
                P.tt("dve", V3, VN3, W00[:, 4 + 2 * hf:6 + 2 * hf].unsqueeze(2).to_broadcast([16, 2, 256]), ALU.add,
                     reads=[tk_, "w00"], writes=[vk])

        pre(0)
        for sl in range(4):
            slab, sk = wl.load(w_in, C_U + sl * 256, 256, scale=GPRE)
            for t in range(17):
                r = rows(t)
                bank = t % 2
                vk = "V%d" % t
                if sl == 0:
                    if t + 1 < 17:
                        pre(t + 1)
                    if t < 16:
                        mix(t)
                proj_tok(PSB[bank][:r, 0:256], slab, sk, 256, t, HT, "HT", psk(bank))
                if sl == 0:
                    if t < 16:
                        mix_evac(t)
                    else:
                        sample_mix()
                Y = Vt(t)[:r, sl * 256:(sl + 1) * 256]
                P.tt("dve", Y, PSB[bank][:r, 0:256], Y, ALU.mult, reads=[psk(bank), vk], writes=[vk])
        for sl in range(4):
            slab, sk = wl.load(w_in, C_ZB + sl * 256, 256, scale=GPRE)
            for t in range(17):
                r = rows(t)
                bank = t % 2
                vk = "V%d" % t
                tm = TMP[t % 2]; tk_ = "tmpG%d" % (t % 2)
                proj_tok(PSB[bank][:r, 0:256], slab, sk, 256, t, HT, "HT", psk(bank))
                if sl == 3 and t >= 1:
                    transpose_tile(Vt(t - 1), "V%d" % (t - 1), t - 1, KT, "KT", 4 + (t - 1) % 2)
                Y = Vt(t)[:r, sl * 256:(sl + 1) * 256]
                P.act(tm[:r, 0:256], PSB[bank][:r, 0:256], AF.Silu, reads=[psk(bank)], writes=[tk_])
                P.tt("dve", Y, Y, tm[:r, 0:256], ALU.mult, reads=[vk, tk_], writes=[vk])
        transpose_tile(Vt(16), "V16", 16, KT, "KT", 4)
    P.barrier()
    if stop_after <= 6:
        return nc

    esw = ExitStack()
    WOf = esw.enter_context(sbt("wo", [128, 8, 1024], BF16))
    WPGf = esw.enter_context(sbt("wpg", [128, 8, 1024], BF16))
    WPLf = esw.enter_context(sbt("wpl", [128, 2, 1024], BF16))
    with ExitStack() as es:
        wl = WL(es, nslab=3)
        SG = [es.enter_context(sbt("sg%d" % i, [128, 512], F32)) for i in range(2)]
        TM2 = es.enter_context(sbt("tm2", [128, 512], F32))
        fin_loads = []
        for sl in range(4):
            fin_loads.append((w_out, sl, 8, WOf, "WO"))
        for sl in range(4):
            fin_loads.append((w_pg, sl, 8, WPGf, "WPG"))
        for sl in range(4):
            fin_loads.append((w_ple, sl, 2, WPLf, "WPL"))
        it = 0
        for br, (wp_, cg, srcT, srck, scl) in enumerate(((w_pa, C_GA, QT, "QT", GHEAD), (w_pb, C_GB, KT, "KT", None))):
            for sl in range(4):
                slab_p, skp = wl.load(wp_, sl * 256, 256, scale=scl)
                slab_g, skg = wl.load(w_in, cg + sl * 256, 256, scale=GPRE)
                for bb in range(2):
                    b = sl * 2 + bb
                    for gi, (t0, n) in enumerate(TOKG):
                        it += 1
                        ba, bg = (it % 2) * 2, (it % 2) * 2 + 1
                        sg = SG[it % 2]
                        sgk = "sg%d" % (it % 2)
                        proj_feat(PSB[bg][:, :n], slab_g, skg, bb * 128, t0, n, HT, "HT", psk(bg))
                        proj_feat(PSB[ba][:, :n], slab_p, skp, bb * 128, t0, n, srcT, srck, psk(ba))
                        P.act(sg[:, :n], PSB[bg][:, :n], AF.Sigmoid, reads=[psk(bg)], writes=[sgk])
                        if br == 0:
                            P.tt("dve", MT[:, b, t0:t0 + n], PSB[ba][:, :n], sg[:, :n], ALU.mult, reads=[psk(ba), sgk], writes=["MT"])
                        else:
                            P.tt("dve", TM2[:, :n], PSB[ba][:, :n], sg[:, :n], ALU.mult, reads=[psk(ba), sgk], writes=["tm2"])
                            P.tt("dve", MT[:, b, t0:t0 + n], MT[:, b, t0:t0 + n], TM2[:, :n], ALU.add, reads=["MT", "tm2"], writes=["MT"])
                for _ in range(2 if fin_loads and len(fin_loads) > 4 else 1):
                    if fin_loads:
                        (w_, sl_, kcn_, dstw, dkey) = fin_loads.pop(0)
                        wl.load(w_, sl_ * 256, 256, kcn=kcn_, dst=(lambda kc, dstw=dstw, sl_=sl_: dstw[:, kc, sl_ * 256:(sl_ + 1) * 256]),
                                dstkey=dkey)
        while fin_loads:
            (w_, sl_, kcn_, dstw, dkey) = fin_loads.pop(0)
            wl.load(w_, sl_ * 256, 256, kcn=kcn_, dst=(lambda kc, dstw=dstw, sl_=sl_: dstw[:, kc, sl_ * 256:(sl_ + 1) * 256]), dstkey=dkey)
    P.barrier()
    if stop_after <= 7:
        return nc

    with ExitStack() as es:
        WO, WPG, WPL = WOf, WPGf, WPLf
        def T(name, shape, dt=F32):
            return es.enter_context(sbt(name, shape, dt))
        def carve(flat, off_elems, n_f32):
            return flat[:, off_elems:off_elems + 2 * n_f32].bitcast(F32)
        GPO = carve(HTf, 0, 1024); GPL = carve(HTf, 2048, 1024)
        XT = [carve(HTf, 4096, 1024), carve(HTf, 6144, 1024)]
        X1 = [carve(HTf, 8192, 1024), carve(HTf, 10240, 1024)]
        YO = [carve(QTf, 0, 1024), carve(QTf, 2048, 1024)]
        SIG = carve(QTf, 4096, 1024)
        PTK = [carve(QTf, 6144, 256), carve(QTf, 6656, 256)]
        X1B = [QTf[:, 8192:9216], QTf[:, 9216:10240]]
        X1T = [QTf[:, 10240:11264].rearrange("p (k n) -> p k n", k=8), QTf[:, 11264:12288].rearrange("p (k n) -> p k n", k=8)]
        PTB = QTf[:, 12288:12544]
        PTT = [QTf[:, 12544:12800].rearrange("p (k n) -> p k n", k=2), QTf[:, 12800:13056].rearrange("p (k n) -> p k n", k=2)]
        JK = KTf[:, 0:1024]
        sf = T("sf", [128, 2, 8])
        P.dma("sp", GPO[:, :], g_post.partition_broadcast(128), writes=["gpo"])
        P.dma("sp", GPL[:, :], g_ple.partition_broadcast(128), writes=["gpo"])

        def load(i):
            r = rows(i); q = i % 2
            P.dma("sp", XT[q][:r, :], xp[i * 128:(i + 1) * 128, :] if i < 16 else xs[:, :], writes=["xtf%d" % q])
            P.dma("sp", PTK[q][:r, :], pp[i * 128:(i + 1) * 128, :] if i < 16 else psd[:, :], writes=["ptk%d" % q])

        def stage_A(i):
            r = rows(i); q = i % 2
            tk = slice(i * 128, i * 128 + r)
            for half in range(2):
                for kc in range(8):
                    P.mm(PSB[half][:r, :], MT[:, kc, tk], WO[:, kc, half * 512:(half + 1) * 512], kc == 0, kc == 7,
                         reads=["MT", "WO"], writes=[psk(half)])

        def stage_B(i):
            r = rows(i); q = i % 2
            s_ = sf[:, q, :]
            sk_ = "sf%d" % q
            for half in range(2):
                P.act(JK[:r, half * 512:(half + 1) * 512], PSB[half][:r, :], AF.Square, reads=[psk(half)], writes=["jkf", sk_],
                      accum_out=s_[:r, half:half + 1])
            P.tt("dve", s_[:r, 0:1], s_[:r, 0:1], s_[:r, 1:2], ALU.add, reads=[sk_], writes=[sk_])
            rstd(s_[:r, 2:3], s_[:r, 0:1], 1024.0, s_[:r, 3:4], [sk_])
            for half in range(2):
                hs = slice(half * 512, (half + 1) * 512)
                P.stt(X1[q][:r, hs], PSB[half][:r, :], s_[:r, 3:4], GPO[:r, hs], ALU.mult, ALU.mult,
                      reads=[psk(half), sk_, "gpo"], writes=["x1%d" % q])
            P.tt("dve", X1[q][:r, :], X1[q][:r, :], XT[q][:r, :], ALU.add, reads=["x1%d" % q, "xtf%d" % q], writes=["x1%d" % q])
            P.cp("pool", X1B[q][:r, :], X1[q][:r, :], reads=["x1%d" % q], writes=["x1b%d" % q])
            P.cp("pool", PTB[:r, :], PTK[q][:r, :], reads=["ptk%d" % q], writes=["ptb"])

        def stage_C(i):
            r = rows(i); q = i % 2
            pv2 = psbf(3).rearrange("p (k n) -> p k n", k=8)
            for kc in range(2):
                P.tr(pv2[:, kc, :r], PTB[:r, kc * 128:(kc + 1) * 128], identb[:r, :r], reads=["ptb", "const"], writes=[psk(3)])
            P.cp("dve", PTT[q][:, :, :r], pv2[:, 0:2, :r], reads=[psk(3)], writes=["ptt%d" % q])
            pv = psbf(2).rearrange("p (k n) -> p k n", k=8)
            for kc in range(8):
                P.tr(pv[:, kc, :r], X1B[q][:r, kc * 128:(kc + 1) * 128], identb[:r, :r], reads=["x1b%d" % q, "const"], writes=[psk(2)])
            P.cp("act", X1T[q][:, :, :r], pv[:, :, :r], reads=[psk(2)], writes=["x1t%d" % q])
            for half in range(2):
                for kc in range(2):
                    P.mm(PSB[6 + half][:r, :], PTT[q][:, kc, :r], WPL[:, kc, half * 512:(half + 1) * 512], kc == 0, kc == 1,
                         reads=["ptt%d" % q, "WPL"], writes=[psk(6 + half)])
            for half in range(2):
                for kc in range(8):
                    P.mm(PSB[4 + half][:r, :], X1T[q][:, kc, :r], WPG[:, kc, half * 512:(half + 1) * 512], kc == 0, kc == 7,
                         reads=["x1t%d" % q, "WPG"], writes=[psk(4 + half)])

        def stage_D(i):
            r = rows(i); q = i % 2
            s_ = sf[:, q, :]
            sk_ = "sg%d" % q
            for half in range(2):
                hs = slice(half * 512, (half + 1) * 512)
                P.act(JK[:r, hs], PSB[6 + half][:r, :], AF.Square, reads=[psk(6 + half)], writes=["jkf", sk_],
                      accum_out=s_[:r, 4 + half:5 + half])
            P.tt("dve", s_[:r, 4:5], s_[:r, 4:5], s_[:r, 5:6], ALU.add, reads=[sk_], writes=[sk_])
            rstd(s_[:r, 6:7], s_[:r, 4:5], 1024.0, s_[:r, 7:8], [sk_])
            for half in range(2):
                hs = slice(half * 512, (half + 1) * 512)
                P.stt(YO[q][:r, hs], PSB[6 + half][:r, :], s_[:r, 7:8], GPL[:r, hs], ALU.mult, ALU.mult,
                      reads=[psk(6 + half), sk_, "gpo"], writes=["yo%d" % q])
            for half in range(2):
                hs = slice(half * 512, (half + 1) * 512)
                P.act(SIG[:r, hs], PSB[4 + half][:r, :], AF.Sigmoid, reads=[psk(4 + half)], writes=["sig"])
            P.tt("dve", YO[q][:r, :], YO[q][:r, :], SIG[:r, :], ALU.mult, reads=["yo%d" % q, "sig"], writes=["yo%d" % q])
            P.tt("dve", YO[q][:r, :], YO[q][:r, :], X1[q][:r, :], ALU.add, reads=["yo%d" % q, "x1%d" % q], writes=["yo%d" % q])
            P.dma("sp", y_p[i * 128:(i + 1) * 128, :] if i < 16 else y_s[:, :], YO[q][:r, :], reads=["yo%d" % q], writes=["y"])

        load(0)
        load(1)
        for i in range(18):
            if i < 17:
                stage_A(i)
            if i >= 1:
                stage_C(i - 1)
            if i < 17:
                stage_B(i)
            if i >= 1:
                stage_D(i - 1)
            if i + 1 < 17 and i >= 1:
                load(i + 1)
    esw.close()
    P.barrier()
    return nc


_NC_CACHE = {}


def _consts():
    bf = ml_dtypes.bfloat16
    s = np.arange(128)[:, None]
    t = np.arange(128)[None, :]
    c = {}
    c["c_identb"] = np.eye(128, dtype=np.float32).astype(bf)
    c["c_identf"] = np.eye(128, dtype=np.float32)
    c["c_negmask"] = np.where(s <= t, 0.0, -30000.0).astype(np.float32).astype(bf)
    c["c_mask01"] = (s <= t).astype(np.float32)
    sel = np.zeros((8, 4, 128), np.float32)
    for h in range(4):
        sel[h, h, :] = 1.0
    c["c_sel8"] = sel.reshape(8, 512)
    sm = np.zeros((8, 16), np.float32)
    sm[4:, 0] = 1.0
    sm[:4, 1] = 1.0
    for k in range(8):
        sm[k, 2 + k % 4] = 1.0
    for k in range(4):
        sm[k, 6 + k] = 1.0
    c["c_small8"] = sm
    c["c_id16b"] = np.broadcast_to(np.eye(16, dtype=np.float32).reshape(1, 256), (128, 256)).copy()
    return c


def kernel(x_prompt, x_sample, state_C, state_n, state_m, state_conv, p_prompt, p_sample,
           g_pre, w_in, b_gates, conv_w, conv_b, g_head, g_lnv, b_lnv, w_s, b_s,
           w_pa, w_pb, w_out, g_post, w_ple, w_ple_gate, g_ple):
    f = lambda a: np.ascontiguousarray(np.asarray(a, dtype=np.float32))
    if "nc" not in _NC_CACHE:
        _NC_CACHE["nc"] = build_program()
    nc = _NC_CACHE["nc"]
    shared = dict(w_in=f(w_in[0]), g_pre=f(g_pre[0]), b_gates=f(b_gates[0]), conv_w=f(conv_w[0]), conv_b=f(conv_b[0]),
                  g_head=f(g_head[0]), g_lnv=f(g_lnv[0]), b_lnv=f(b_lnv[0]), w_s=f(w_s[0]), b_s=f(b_s[0]),
                  w_pa=f(w_pa[0]), w_pb=f(w_pb[0]), w_out=f(w_out[0]), g_post=f(g_post[0]), w_ple=f(w_ple[0]),
                  w_pg=f(w_ple_gate[0]), g_ple=f(g_ple[0]))
    shared.update(_consts())
    in_maps = []
    for c in range(8):
        sl = slice(16 * c, 16 * c + 16)
        d = dict(shared)
        d.update(xp=f(x_prompt[c]), xs=f(x_sample[sl, 0]), pp=f(p_prompt[0, c]), ps=f(p_sample[0, sl, 0]),
                 sC=f(state_C[0, sl]), sn=f(np.asarray(state_n[0, sl]).reshape(16, 1024)), sm=f(state_m[0, sl]),
                 sconv=f(state_conv[0, sl]))
        in_maps.append(d)
    res = run_bass_kernel_spmd(nc, in_maps, core_ids=list(range(8)))
    R = res.results
    cat = lambda k: np.concatenate([np.asarray(r[k]) for r in R], axis=0)
    stk = lambda k: np.stack([np.asarray(r[k]) for r in R], axis=0)
    y_prompt = stk("y_p")
    y_sample = cat("y_s").reshape(128, 1, 1024)
    C_prompt = stk("C_p")[None]
    n_prompt = stk("n_p")[None]
    m_prompt = stk("m_p").reshape(1, 8, 4)
    conv_prompt = stk("conv_p")[None]
    C_sample = cat("C_s")[None]
    n_sample = cat("n_s").reshape(1, 128, 4, 256)
    m_sample = cat("m_s")[None]
    conv_sample = cat("conv_s")[None]
    chunk_v = cat("vn_s").reshape(1, 128, 1, 1024)
    outs = (y_prompt, y_sample, C_prompt, n_prompt, m_prompt, conv_prompt, C_sample, n_sample, m_sample,
            conv_sample, chunk_v)
    return tuple(np.ascontiguousarray(o, dtype=np.float32) for o in outs)
```
